# Optimizing a Trainium2 kernel written in Bass

```python
import jax, jax.numpy as jnp
from jax import lax
import numpy as np

D_MODEL = 1024
BATCH = 4
SEQ = 8192
DEPTH = 2
DEC_BATCH = 128
DEC_SEQ = 1
PAST_LEN = 16384
PAGE_SIZE = 128

CONV_W = D_MODEL
CONV_K = 3
POOL_W = D_MODEL
POOL_WINDOWS = (2, 4, 8, 16)
POOL_GROUPS = len(POOL_WINDOWS)
POOL_GROUP_W = POOL_W // POOL_GROUPS
POOL_MAX = max(POOL_WINDOWS)
HEAD_DIM = 64
N_HEADS = D_MODEL // HEAD_DIM
N_KV_HEADS = 4
GQA_GROUP = N_HEADS // N_KV_HEADS
WINDOW = 128
BLOCK = 128
ROPE_THETA = 10000.0
D_FF = 4 * D_MODEL
RMS_EPS = 1e-6
NEG = -1e30

SPLITS = (CONV_W, CONV_W, CONV_W, POOL_W,
          N_HEADS * HEAD_DIM, N_KV_HEADS * HEAD_DIM, N_KV_HEADS * HEAD_DIM,
          D_MODEL, D_MODEL, D_MODEL)
D_IN_PROJ = sum(SPLITS)
SPLIT_IDX = tuple(sum(SPLITS[:i + 1]) for i in range(len(SPLITS) - 1))

kernel_name = 'hybrid_conv_pool_swa_decoder_step'


def rmsnorm(x, g):
    xf = x.astype(jnp.float32)
    y = xf * lax.rsqrt(jnp.mean(xf * xf, axis=-1, keepdims=True) + RMS_EPS)
    return (y * g.astype(jnp.float32)).astype(x.dtype)


def rope(x, pos):
    half = HEAD_DIM // 2
    inv = ROPE_THETA ** (-jnp.arange(half, dtype=jnp.float32) / half)
    ang = pos.astype(jnp.float32)[:, None] * inv[None, :]
    cos = jnp.cos(ang)[:, None, :]
    sin = jnp.sin(ang)[:, None, :]
    xf = x.astype(jnp.float32)
    x1, x2 = xf[..., :half], xf[..., half:]
    out = jnp.concatenate([x1 * cos - x2 * sin, x2 * cos + x1 * sin], axis=-1)
    return out.astype(x.dtype)


def short_conv(u, past, w):
    T = u.shape[1]
    ext = jnp.concatenate([past, u], axis=1)
    out = w[0] * ext[:, 0:T] + w[1] * ext[:, 1:T + 1] + w[2] * ext[:, 2:T + 2]
    return out, ext[:, T:]


def multiscale_pool(u, past, start):
    T = u.shape[1]
    P = POOL_MAX - 1
    ext = jnp.concatenate([past, u], axis=1)
    cs = jnp.cumsum(ext.astype(jnp.float32), axis=1)
    cs = jnp.pad(cs, ((0, 0), (1, 0), (0, 0)))
    pos = start + jnp.arange(T, dtype=jnp.int32)
    outs = []
    for g, win in enumerate(POOL_WINDOWS):
        lo, hi = g * POOL_GROUP_W, (g + 1) * POOL_GROUP_W
        s = cs[:, P + 1:P + 1 + T, lo:hi] - cs[:, P + 1 - win:P + 1 - win + T, lo:hi]
        cnt = jnp.minimum(win, pos + 1).astype(jnp.float32)[None, :, None]
        outs.append(s / cnt - u[..., lo:hi].astype(jnp.float32))
    pooled = jnp.concatenate(outs, axis=-1).astype(u.dtype)
    return pooled, ext[:, T:]


def window_mask(qpos, kpos):
    d = qpos[..., :, None] - kpos[..., None, :]
    return (d >= 0) & (d < WINDOW) & (kpos[..., None, :] >= 0)


def sink_attend(q, k, v, mask, sinks):
    s = jnp.einsum('...qhgd,...khd->...hgqk', q, k,
                   preferred_element_type=jnp.float32) * (HEAD_DIM ** -0.5)
    s = jnp.where(mask[..., None, None, :, :], s, NEG)
    sk = sinks.astype(jnp.float32).reshape(N_KV_HEADS, GQA_GROUP)[:, :, None, None]
    m = jnp.maximum(jnp.max(s, axis=-1, keepdims=True), sk)
    p = jnp.exp(s - m)
    den = jnp.sum(p, axis=-1, keepdims=True) + jnp.exp(sk - m)
    p = (p / den).astype(v.dtype)
    return jnp.einsum('...hgqk,...khd->...qhgd', p, v)


def swa_banded(q, k, v, sinks):
    B, T = q.shape[:2]
    nb = T // BLOCK
    qb = q.reshape(B, nb, BLOCK, N_KV_HEADS, GQA_GROUP, HEAD_DIM)
    kpad = jnp.pad(k, ((0, 0), (BLOCK, 0), (0, 0), (0, 0)))[:, :T]
    vpad = jnp.pad(v, ((0, 0), (BLOCK, 0), (0, 0), (0, 0)))[:, :T]
    kb = jnp.concatenate([kpad.reshape(B, nb, BLOCK, N_KV_HEADS, HEAD_DIM),
                          k.reshape(B, nb, BLOCK, N_KV_HEADS, HEAD_DIM)], axis=2)
    vb = jnp.concatenate([vpad.reshape(B, nb, BLOCK, N_KV_HEADS, HEAD_DIM),
                          v.reshape(B, nb, BLOCK, N_KV_HEADS, HEAD_DIM)], axis=2)
    qpos = jnp.arange(T, dtype=jnp.int32).reshape(nb, BLOCK)
    kpos = (jnp.arange(nb, dtype=jnp.int32) * BLOCK - BLOCK)[:, None] + jnp.arange(2 * BLOCK, dtype=jnp.int32)[None, :]
    o = sink_attend(qb, kb, vb, window_mask(qpos, kpos), sinks)
    return o.reshape(B, T, N_HEADS * HEAD_DIM)


def swa_with_buffer(q, k, v, k_past, v_past, start, sinks):
    B, T = q.shape[:2]
    sw = k_past.shape[1]
    kc = jnp.concatenate([k_past, k], axis=1)
    vc = jnp.concatenate([v_past, v], axis=1)
    qpos = start + jnp.arange(T, dtype=jnp.int32)
    kpos = start - sw + jnp.arange(sw + T, dtype=jnp.int32)
    o = sink_attend(q.reshape(B, T, N_KV_HEADS, GQA_GROUP, HEAD_DIM), kc, vc,
                    window_mask(qpos, kpos), sinks)
    return o.reshape(B, T, N_HEADS * HEAD_DIM), kc[:, T:], vc[:, T:]


def hybrid_layer(x, conv_past, pool_past, k_past, v_past, start, sw_buf, lp):
    (w_in, conv_w, w_conv_out, w_pool, pool_scale, sinks, w_attn_out, w_mix_out,
     g_pre_mix, g_post_mix, g_pre_mlp, g_post_mlp, w_up, w_down) = lp
    B, T, _ = x.shape
    h = rmsnorm(x, g_pre_mix)
    proj = h @ w_in
    hc, bg, cg, up, q, k, v, gc, gp, ga = jnp.split(proj, SPLIT_IDX, axis=-1)
    conv_out, conv_new = short_conv(cg * hc, conv_past, conv_w)
    y_conv = (bg * conv_out) @ w_conv_out
    pooled, pool_new = multiscale_pool(up, pool_past, start)
    y_pool = jnp.einsum('btgc,gcd->btgd', pooled.reshape(B, T, POOL_GROUPS, POOL_GROUP_W),
                        w_pool).reshape(B, T, POOL_W) * pool_scale
    pos = start + jnp.arange(T, dtype=jnp.int32)
    q = rope(q.reshape(B, T, N_HEADS, HEAD_DIM), pos)
    k = rope(k.reshape(B, T, N_KV_HEADS, HEAD_DIM), pos)
    v = v.reshape(B, T, N_KV_HEADS, HEAD_DIM)
    if k_past is None:
        attn = swa_banded(q, k, v, sinks)
        k_new, v_new = k[:, T - sw_buf:], v[:, T - sw_buf:]
    else:
        attn, k_new, v_new = swa_with_buffer(q, k, v, k_past, v_past, start, sinks)
    y_attn = attn @ w_attn_out
    merged = jax.nn.sigmoid(gc) * y_conv + jax.nn.sigmoid(gp) * y_pool + jax.nn.sigmoid(ga) * y_attn
    x = x + rmsnorm(merged @ w_mix_out, g_post_mix)
    f = jnp.square(jax.nn.relu(rmsnorm(x, g_pre_mlp) @ w_up)) @ w_down
    x = x + rmsnorm(f, g_post_mlp)
    return x, conv_new, pool_new, k_new, v_new


def setup_inputs(seed: int = 0) -> dict:
    key = jax.random.key(seed)
    ks = jax.random.split(key, 22)
    f32 = jnp.float32

    def nrm(k, shape, scale):
        return jax.random.normal(k, shape, f32) * scale

    sw_buf = min(WINDOW, PAST_LEN)
    return {
        'x_prompt': nrm(ks[0], (BATCH, SEQ, D_MODEL), 1.0),
        'x_sample': nrm(ks[1], (DEC_BATCH, DEC_SEQ, D_MODEL), 1.0),
        'state_conv': nrm(ks[2], (DEPTH, DEC_BATCH, CONV_K - 1, CONV_W), 1.0),
        'state_pool': nrm(ks[3], (DEPTH, DEC_BATCH, POOL_MAX - 1, POOL_W), 1.0),
        'cache_k': nrm(ks[4], (DEPTH, DEC_BATCH, sw_buf, N_KV_HEADS, HEAD_DIM), 1.0),
        'cache_v': nrm(ks[5], (DEPTH, DEC_BATCH, sw_buf, N_KV_HEADS, HEAD_DIM), 1.0),
        'w_in': nrm(ks[6], (DEPTH, D_MODEL, D_IN_PROJ), D_MODEL ** -0.5),
        'conv_w': nrm(ks[7], (DEPTH, CONV_K, CONV_W), CONV_K ** -0.5),
        'w_conv_out': nrm(ks[8], (DEPTH, CONV_W, D_MODEL), CONV_W ** -0.5),
        'w_pool': nrm(ks[9], (DEPTH, POOL_GROUPS, POOL_GROUP_W, POOL_GROUP_W), POOL_GROUP_W ** -0.5),
        'pool_scale': 1.0 + nrm(ks[10], (DEPTH, POOL_W), 0.1),
        'attn_sinks': nrm(ks[11], (DEPTH, N_HEADS), 0.5),
        'w_attn_out': nrm(ks[12], (DEPTH, N_HEADS * HEAD_DIM, D_MODEL), (N_HEADS * HEAD_DIM) ** -0.5),
        'w_mix_out': nrm(ks[13], (DEPTH, D_MODEL, D_MODEL), D_MODEL ** -0.5),
        'norm_pre_mix': 1.0 + nrm(ks[14], (DEPTH, D_MODEL), 0.05),
        'norm_post_mix': 1.0 + nrm(ks[15], (DEPTH, D_MODEL), 0.05),
        'norm_pre_mlp': 1.0 + nrm(ks[16], (DEPTH, D_MODEL), 0.05),
        'norm_post_mlp': 1.0 + nrm(ks[17], (DEPTH, D_MODEL), 0.05),
        'w_up': nrm(ks[18], (DEPTH, D_MODEL, D_FF), D_MODEL ** -0.5),
        'w_down': nrm(ks[19], (DEPTH, D_FF, D_MODEL), D_FF ** -0.5),
    }


def reference(x_prompt, x_sample, state_conv, state_pool, cache_k, cache_v,
              w_in, conv_w, w_conv_out, w_pool, pool_scale, attn_sinks, w_attn_out,
              w_mix_out, norm_pre_mix, norm_post_mix, norm_pre_mlp, norm_post_mlp,
              w_up, w_down):
    sw_buf = cache_k.shape[2]
    bp = x_prompt.shape[0]
    xp, xs = x_prompt, x_sample
    pc, pp, pk, pv = [], [], [], []
    sc, sp, sk, sv = [], [], [], []
    for l in range(DEPTH):
        lp = (w_in[l], conv_w[l], w_conv_out[l], w_pool[l], pool_scale[l], attn_sinks[l],
              w_attn_out[l], w_mix_out[l], norm_pre_mix[l], norm_post_mix[l],
              norm_pre_mlp[l], norm_post_mlp[l], w_up[l], w_down[l])
        zc = jnp.zeros((bp, CONV_K - 1, CONV_W), xp.dtype)
        zp = jnp.zeros((bp, POOL_MAX - 1, POOL_W), xp.dtype)
        xp, c_new, p_new, k_new, v_new = hybrid_layer(xp, zc, zp, None, None, 0, sw_buf, lp)
        pc.append(c_new); pp.append(p_new); pk.append(k_new); pv.append(v_new)
        xs, c_new, p_new, k_new, v_new = hybrid_layer(xs, state_conv[l], state_pool[l],
                                                      cache_k[l], cache_v[l], PAST_LEN, sw_buf, lp)
        sc.append(c_new); sp.append(p_new); sk.append(k_new); sv.append(v_new)
    return (xp, xs,
            jnp.stack(pc), jnp.stack(pp), jnp.stack(pk), jnp.stack(pv),
            jnp.stack(sc), jnp.stack(sp), jnp.stack(sk), jnp.stack(sv))
```

```python
import numpy as np
import concourse.bass as bass
import concourse.mybir as mybir
from concourse.bass_utils import run_bass_kernel_spmd

F32 = mybir.dt.float32
BF16 = mybir.dt.bfloat16
AF = mybir.ActivationFunctionType
ALU = mybir.AluOpType
AX = mybir.AxisListType

D = 1024
DFF = 4096
DIN = 8704
NCORES = 8
NS = 16
SEQ = 8192
NBLK_FULL = 33
EPS = 1e-6
POOL_WINS = (2, 4, 8, 16)


class Buf:
    __slots__ = ("name", "w", "r", "excl")

    def __init__(self, name, excl=False):
        self.name = name
        self.w = None
        self.r = {}
        self.excl = excl


class DmaSem:
    def __init__(self, nc, name):
        self.h = nc.alloc_semaphore(name)
        self.key = name
        self.val = 0
        self.last = None


class Sched:
    LIMIT = 20000

    def __init__(self, nc):
        self.nc = nc
        self.eng = {"pe": nc.tensor, "act": nc.scalar, "dve": nc.vector, "pool": nc.gpsimd, "sp": nc.sync}
        self.sem = {}
        self.cnt = {}
        self.gen = {}
        for e in self.eng:
            self.gen[e] = 0
            self.sem[e] = nc.alloc_semaphore(f"s_{e}_0")
            self.cnt[e] = 0
        self.seen = {e: {} for e in self.eng}
        self.ninst = {e: 0 for e in self.eng}

    def _key(self, e):
        return f"{e}{self.gen[e]}"

    def _wait(self, e, tok, raw):
        if tok is None:
            return
        key, h, val, src = tok
        if src == e and e == "pe":
            return
        if self.seen[e].get(key, 0) >= val:
            return
        self.eng[e].wait_ge(h, val)
        self.seen[e][key] = val

    def _deps(self, e, reads, writes):
        for b in reads:
            self._wait(e, b.w, True)
            if b.excl:
                for t in b.r.values():
                    if t[3] != e:
                        self._wait(e, t, False)
        for b in writes:
            self._wait(e, b.w, True)
            for t in b.r.values():
                self._wait(e, t, False)

    @staticmethod
    def _addr(b, tok):
        old = b.r.get(tok[0])
        if old is None or old[2] < tok[2]:
            b.r[tok[0]] = tok

    def _record(self, tok, reads, writes):
        for b in reads:
            self._addr(b, tok)
        for b in writes:
            b.w = tok
            b.r = {}

    def op(self, e, fn, reads=(), writes=(), signal=True):
        self._deps(e, reads, writes)
        inst = fn(self.eng[e])
        self.ninst[e] += 1
        if signal:
            self.cnt[e] += 1
            inst.then_inc(self.sem[e], 1)
            tok = (self._key(e), self.sem[e], self.cnt[e], e)
            if self.cnt[e] >= self.LIMIT:
                self.gen[e] += 1
                self.sem[e] = self.nc.alloc_semaphore(f"s_{e}_{self.gen[e]}")
                self.cnt[e] = 0
        else:
            tok = (self._key(e), self.sem[e], self.cnt[e] + 1, e)
        self._record(tok, reads, writes)
        return tok

    def dma(self, q, dsem, out, in_, reads=(), writes=(), **kw):
        self._deps(q, reads, writes)
        if dsem.last is not None:
            self._wait(q, dsem.last, True)
        inst = self.eng[q].dma_start(out=out, in_=in_, **kw)
        dsem.val += 16
        inst.then_inc(dsem.h, 16)
        tok = (dsem.key, dsem.h, dsem.val, None)
        dsem.last = tok
        self._record(tok, reads, writes)
        return tok

    def alias(self, srcs, dsts):
        for d in dsts:
            for s_ in srcs:
                if s_.w is not None:
                    self._addr(d, s_.w)
                for t in s_.r.values():
                    self._addr(d, t)


class Ring:
    def __init__(self, nc, name, shape, dtype, n):
        self.t = [nc.alloc_sbuf_tensor(f"{name}{i}", shape, dtype) for i in range(n)]
        self.b = [Buf(f"{name}{i}") for i in range(n)]
        self.i = 0
        self.n = n

    def get(self):
        i = self.i
        self.i = (self.i + 1) % self.n
        return self.t[i], self.b[i]


class Group:
    def __init__(self, kind, blocks, c0, n, first_chain=False):
        self.kind = kind
        self.blocks = blocks
        self.c0 = c0
        self.n = n
        self.first_chain = first_chain


def build_program(nblk, tiles, with_samples=True, depth=2, nslot=5):
    nc = bass.Bass("TRN2", target_bir_lowering=False)
    S = Sched(nc)
    L = depth
    NBS = max(nb + (2 if (hs and with_samples) else 0) for nb, hs in tiles)
    TT = max(nb * 128 + (NS if hs else 0) for nb, hs in tiles)
    TT = max([TT] + [128 * (nb + 1) for nb, hs in tiles if hs and with_samples])
    TT = (TT + 7) // 8 * 8

    def din(name, shape, dt=F32):
        return nc.dram_tensor(name, list(shape), dt, kind="ExternalInput").ap()

    def dout(name, shape, dt=F32):
        return nc.dram_tensor(name, list(shape), dt, kind="ExternalOutput").ap()

    xin = din("xin", [nblk * 128, D])
    y = dout("y", [nblk * 128, D])
    w_in = din("w_in", [L, D, DIN])
    conv_w = din("conv_w", [L, 3, D])
    w_conv_out = din("w_conv_out", [L, D, D])
    w_pool = din("w_pool", [L, 4, 256, 256])
    pool_scale = din("pool_scale", [L, D])
    attn_sinks = din("attn_sinks", [L, 16])
    w_attn_out = din("w_attn_out", [L, D, D])
    w_mix_out = din("w_mix_out", [L, D, D])
    g_pre_mix = din("norm_pre_mix", [L, D])
    g_post_mix = din("norm_post_mix", [L, D])
    g_pre_mlp = din("norm_pre_mlp", [L, D])
    g_post_mlp = din("norm_post_mlp", [L, D])
    w_up = din("w_up", [L, D, DFF])
    w_down = din("w_down", [L, DFF, D])
    rope_cos = din("rope_cos", [128, nblk * 128])
    rope_sin = din("rope_sin", [128, nblk * 128])
    c_maskc = din("c_maskc", [128, 512])
    c_maskp = din("c_maskp", [128, 512])
    c_rot = din("c_rot", [128, 128])
    c_ident = din("c_ident", [128, 128])
    c_invcnt = din("c_invcnt", [128, 8 * 16])
    pconv_o = dout("pconv_o", [L, 2, D])
    ppool_o = dout("ppool_o", [L, 15, D])
    pk_o = dout("pk_o", [L, 128, 256])
    pv_o = dout("pv_o", [L, 128, 256])
    if with_samples:
        xs = din("xs", [NS, D])
        ys = dout("ys", [NS, D])
        sconv = din("sconv", [L, NS, 2, D])
        spool = din("spool", [L, NS, 15, D])
        ck = din("ck", [L, NS, 128, 256])
        cv = din("cv", [L, NS, 128, 256])
        rope_s = din("rope_s", [128, 2])
        sconv_o = dout("sconv_o", [L, NS, 2, D])
        spool_o = dout("spool_o", [L, NS, 15, D])
        sk_o = dout("sk_o", [L, NS, 128, 256])
        sv_o = dout("sv_o", [L, NS, 128, 256])

    sb = nc.alloc_sbuf_tensor
    x = sb("x", [128, NBS, D], F32)
    xB = [Buf(f"x{b}") for b in range(NBS)]
    xsem = [DmaSem(nc, f"dx{b}") for b in range(NBS)]
    hT = sb("hT", [128, 8, TT], BF16)
    big = sb("big", [128, 32 * TT], BF16)
    FT = big[:, :].rearrange("p (f t) -> p f t", f=32)
    acc = big[:, 0:16 * TT].bitcast(F32).rearrange("p (c t) -> p c t", c=8)
    aT = big[:, 16 * TT:24 * TT].rearrange("p (c t) -> p c t", c=8)
    QT = big[:, 24 * TT:32 * TT].rearrange("p (c t) -> p c t", c=8)
    MAXG = 3
    aT_flat = big[:, 16 * TT:24 * TT]
    acc_flat = big[:, 0:16 * TT]
    NHT = (8 * TT) // 1024
    rs_keep = sb("rs_keep", [128, 8], F32)
    rskB = [Buf(f"rsk{b}") for b in range(8)]
    htmQB = [Buf(f"htmq{b}") for b in range(NHT)]
    junkQB = [Buf(f"junkq{b}") for b in range(NHT)]
    hTB = [Buf(f"hT{g}") for g in range(MAXG)]
    accB = [Buf(f"acc{g}") for g in range(MAXG)]
    aTB = [Buf(f"aT{g}") for g in range(MAXG)]
    QTB = [Buf(f"QT{g}") for g in range(MAXG)]
    FTB = [Buf(f"FT{g}") for g in range(MAXG)]

    wslot = [sb(f"wslot{i}", [128, 8, 512], BF16) for i in range(nslot)]
    wB = [Buf(f"w{i}") for i in range(nslot)]
    wsem = [DmaSem(nc, f"dw{i}") for i in range(nslot)]

    ps = nc.alloc_psum_tensor("ps", [128, 8, 512], F32)
    pB = [Buf(f"ps{i}", excl=True) for i in range(8)]

    identb = sb("identb", [128, 128], BF16)
    identf = sb("identf", [128, 128], F32)
    rotb = sb("rotb", [128, 128], BF16)
    maskc = sb("maskc", [128, 512], BF16)
    maskp = sb("maskp", [128, 512], BF16)
    invcnt = sb("invcnt", [128, 8, 16], F32)
    gpreT = sb("gpreT", [128, L, 8], F32)
    gmlpT = sb("gmlpT", [128, L, 8], F32)
    pscT = sb("pscT", [128, L, 8], F32)
    cwT = sb("cwT", [128, L, 3, 8], F32)
    sinkexp = sb("sinkexp", [128, L, 16], F32)
    gpost = sb("gpost", [128, 2, D], F32)
    chalf = sb("chalf", [128, 1], F32)
    constB = Buf("const")
    gpostB = Buf("gpost")
    gpsem = DmaSem(nc, "dgp")
    csem = [DmaSem(nc, f"dc{i}") for i in range(4)]

    cstate = sb("cstate", [128, L, 8, 2], F32)
    pstate = sb("pstate", [128, L, 8, 15], F32)
    cstB = [[Buf(f"cst{l}_{j}") for j in range(8)] for l in range(L)]
    pstB = [[Buf(f"pst{l}_{j}") for j in range(8)] for l in range(L)]
    KTbuf = [sb(f"KT{l}", [128, 4, 640], BF16) for l in range(L)]
    KTB = [Buf(f"KT{l}") for l in range(L)]
    Vbuf = [sb(f"V{l}", [128, 5, 4, 65], BF16) for l in range(L)]
    VB = [[Buf(f"V{l}_{i}") for i in range(5)] for l in range(L)]
    kfin = sb("kfin", [128, 4, 128], F32)
    kfinB = Buf("kfin")
    vfin = sb("vfin", [128, 256], F32)
    vfinB = Buf("vfin")
    if with_samples:
        scst = sb("scst", [128, 8, 32], F32)
        scstB = Buf("scst")
        unew = sb("unew", [128, 8, NS], F32)
        unewB = Buf("unew")
        upnew = sb("upnew", [128, 8, NS], F32)
        upnewB = Buf("upnew")
        kfs = sb("kfs", [128, 4, NS], F32)
        kfsB = Buf("kfs")
        ones64 = sb("ones64", [128, 64], BF16)
        sinkT = sb("sinkT", [128, L, 8], F32)
        spB_ = Buf("spstate")
        skoB = [Buf(f"sko{l}") for l in range(L)]
        svoB = [Buf(f"svo{l}") for l in range(L)]
        ssem = [DmaSem(nc, f"dsm{i}") for i in range(10)]

    R528 = Ring(nc, "r528_", [128, 528], F32, 6)
    RB512 = Ring(nc, "rb512_", [128, 512], BF16, 9)
    RB1024 = Ring(nc, "rb1024_", [128, 1024], BF16, 3)
    RF1024 = Ring(nc, "rf1024_", [128, 1024], F32, 1)
    RS = Ring(nc, "rs_", [128, 4], F32, 16)
    RS16 = Ring(nc, "rs16_", [128, 16], F32, 4)
    ropeT = sb("ropeT", [128, 2, 512], F32)
    ropeB = Buf("rope")
    ropesem = DmaSem(nc, "drope")
    osem = [DmaSem(nc, f"do{i}") for i in range(4)]
    final_toks = []

    pstate_rr = {"i": 0, "avoid": ()}

    def bank():
        i = pstate_rr["i"]
        while i in pstate_rr["avoid"]:
            i = (i + 1) % 8
        pstate_rr["i"] = (i + 1) % 8
        return i

    def bank_pair():
        i = pstate_rr["i"]
        if i % 2:
            i = (i + 1) % 8
        pstate_rr["i"] = (i + 2) % 8
        return i

    def mm(out, lhsT, rhs, start, stop, reads, writes, signal, **kw):
        return S.op("pe", lambda e: e.matmul(out, lhsT, rhs, start=start, stop=stop, **kw),
                    reads=reads, writes=writes, signal=signal)

    def tr(out, in_, ident, reads, writes, signal):
        return S.op("pe", lambda e: e.transpose(out, in_, ident), reads=reads, writes=writes, signal=signal)

    def act(out, in_, func, reads, writes, **kw):
        return S.op("act", lambda e: e.activation(out=out, in_=in_, func=func, **kw), reads=reads, writes=writes)

    def tt(eng, out, in0, in1, op, reads, writes):
        return S.op(eng, lambda e: e.tensor_tensor(out=out, in0=in0, in1=in1, op=op), reads=reads, writes=writes)

    def ts(eng, out, in0, s1, s2, op0, op1, reads, writes):
        if op1 is None:
            return S.op(eng, lambda e: e.tensor_scalar(out=out, in0=in0, scalar1=s1, scalar2=None, op0=op0),
                        reads=reads, writes=writes)
        return S.op(eng, lambda e: e.tensor_scalar(out=out, in0=in0, scalar1=s1, scalar2=s2, op0=op0, op1=op1),
                    reads=reads, writes=writes)

    def stt(out, in0, scalar, in1, op0, op1, reads, writes):
        return S.op("dve", lambda e: e.scalar_tensor_tensor(out=out, in0=in0, scalar=scalar, in1=in1, op0=op0, op1=op1),
                    reads=reads, writes=writes)

    def cp(eng, out, in_, reads, writes):
        if eng == "act":
            return act(out, in_, AF.Copy, reads, writes)
        return S.op(eng, lambda e: e.tensor_copy(out=out, in_=in_), reads=reads, writes=writes)

    preloaded = set()
    for b in range(tiles[0][0]):
        S.dma("sp", xsem[b], x[:, b, :], xin[b * 128:(b + 1) * 128, :], writes=[xB[b]])
        preloaded.add((0, b))

    def load_const_f32(dst, src, sem, **kw):
        S.dma("sp", sem, dst, src, writes=[constB], **kw)

    tmpc, tmpcB = R528.get()
    load_const_f32(identf[:, :], c_ident[:, :], csem[0])
    cp("dve", identb[:, :], identf[:, :], [constB], [constB])
    S.dma("sp", csem[1], tmpc[:, 0:128], c_rot[:, :], writes=[tmpcB])
    cp("dve", rotb[:, :], tmpc[:, 0:128], [tmpcB], [constB])
    tmpc2, tmpc2B = R528.get()
    S.dma("sp", csem[2], tmpc2[:, 0:512], c_maskc[:, :], writes=[tmpc2B])
    cp("dve", maskc[:, :], tmpc2[:, 0:512], [tmpc2B], [constB])
    tmpc3, tmpc3B = R528.get()
    S.dma("sp", csem[3], tmpc3[:, 0:512], c_maskp[:, :], writes=[tmpc3B])
    cp("dve", maskp[:, :], tmpc3[:, 0:512], [tmpc3B], [constB])
    load_const_f32(invcnt[:, :, :].rearrange("p a b -> p (a b)"), c_invcnt[:, :], csem[0])
    for l in range(L):
        load_const_f32(gpreT[:, l, :], g_pre_mix[l, :].rearrange("(c p) -> p c", p=128), csem[1],
                       allow_slow_non_contiguous=True)
        load_const_f32(gmlpT[:, l, :], g_pre_mlp[l, :].rearrange("(c p) -> p c", p=128), csem[2],
                       allow_slow_non_contiguous=True)
        load_const_f32(pscT[:, l, :], pool_scale[l, :].rearrange("(c p) -> p c", p=128), csem[3],
                       allow_slow_non_contiguous=True)
        for t in range(3):
            load_const_f32(cwT[:, l, t, :], conv_w[l, t, :].rearrange("(c p) -> p c", p=128), csem[0],
                           allow_slow_non_contiguous=True)
        load_const_f32(sinkexp[:, l, :], attn_sinks[l:l + 1, :].partition_broadcast(128).rearrange("p a b -> p (a b)"),
                       csem[1])
    act(sinkexp[:, :, :], sinkexp[:, :, :], AF.Exp, [constB], [constB])
    S.op("dve", lambda e: e.memset(chalf[:, :], -0.5), writes=[constB])
    if with_samples:
        S.op("dve", lambda e: e.memset(ones64[:, :], 1.0), writes=[constB])
        for l in range(L):
            for half in range(2):
                load_const_f32(sinkT[half * 64:(half + 1) * 64, l, :],
                               attn_sinks[l, :].rearrange("(c two) -> two c", two=2)[half:half + 1, :]
                               .partition_broadcast(64).rearrange("p a b -> p (a b)"),
                               csem[2 + half], allow_slow_non_contiguous=True)
        act(sinkT[:, :, :], sinkT[:, :, :], AF.Exp, [constB], [constB])
    S.op("dve", lambda e: e.memset(cstate[:, :, :, :].rearrange("p a b c -> p (a b c)"), 0.0),
         writes=[b for l in range(L) for b in cstB[l]])
    S.op("dve", lambda e: e.memset(pstate[:, :, :, :].rearrange("p a b c -> p (a b c)"), 0.0),
         writes=[b for l in range(L) for b in pstB[l]])
    for l in range(L):
        S.op("dve", lambda e: e.memset(Vbuf[l][:, :, :, :].rearrange("p a b c -> p (a b c)"), 1.0), writes=VB[l])
        S.op("dve", lambda e: e.memset(KTbuf[l][:, :, :].rearrange("p a b -> p (a b)"), 0.0), writes=[KTB[l]])

    if with_samples:
        for l in range(L):
            S.dma("sp", ssem[2], sk_o[l, :, 0:127, :], ck[l, :, 1:128, :], writes=[skoB[l]])
            S.dma("sp", ssem[3], sv_o[l, :, 0:127, :], cv[l, :, 1:128, :], writes=[svoB[l]])

    def piece_list(l):
        P = []
        for h in range(2):
            P += [("hc", h), ("cg", h), ("bg", h)]
        for h in range(2):
            P += [("wco", h), ("gc", h)]
        P += [("up", 0), ("up", 1), ("wpool", 0), ("gp", 0), ("gp", 1)]
        P += [("q", 0), ("q", 1), ("kv", 0), ("kdup", 0)]
        for h in range(2):
            P += [("wao", h), ("ga", h)]
        P += [("wmix", 0), ("wmix", 1)]
        P += [("wup", i) for i in range(8)]
        P += [("wdn", (cgi, kq)) for cgi in range(2) for kq in range(4)]
        return [(l, k, i) for (k, i) in P]

    pieces = []
    for ti in range(len(tiles)):
        for l in range(L):
            pieces += [(ti,) + p for p in piece_list(l)]
    pidx = {p: i for i, p in enumerate(pieces)}
    issued = {"n": 0}
    COL0 = {"hc": 0, "bg": 1024, "cg": 2048, "up": 3072, "q": 4096, "gc": 5632, "gp": 6656, "ga": 7680}

    def issue_piece(i):
        ti, l, kind, idx = pieces[i]
        s = i % nslot
        dst = wslot[s]

        def d(out, in_):
            S.dma("pool", wsem[s], out, in_, writes=[wB[s]])
        if kind in COL0:
            c0 = COL0[kind] + idx * 512
            d(dst[:, :, :], w_in[l, :, c0:c0 + 512].rearrange("(kc p) c -> p kc c", p=128))
        elif kind == "kdup":
            pass
        elif kind == "kv":
            d(dst[:, :, :], w_in[l, :, 5120:5632].rearrange("(kc p) c -> p kc c", p=128))
        elif kind in ("wco", "wao", "wmix"):
            wsrc = {"wco": w_conv_out, "wao": w_attn_out, "wmix": w_mix_out}[kind]
            d(dst[:, :, :], wsrc[l, :, idx * 512:(idx + 1) * 512].rearrange("(kc p) c -> p kc c", p=128))
        elif kind == "wpool":
            d(dst[:, :, 0:256], w_pool[l].rearrange("g (kc p) c -> p (g kc) c", p=128))
        elif kind == "wup":
            d(dst[:, :, :], w_up[l, :, idx * 512:(idx + 1) * 512].rearrange("(kc p) c -> p kc c", p=128))
        elif kind == "wdn":
            cgi, kq = idx
            d(dst[:, :, :], w_down[l, kq * 1024:(kq + 1) * 1024, cgi * 512:(cgi + 1) * 512]
              .rearrange("(kc p) c -> p kc c", p=128))
        else:
            raise ValueError(kind)

    def need(ti, l, names):
        idxs = [pidx[(ti, l) + nm] for nm in names]
        lo = min(idxs)
        hi = min(len(pieces), lo + nslot)
        assert max(idxs) < hi, "ring too small"
        while issued["n"] < hi:
            issue_piece(issued["n"])
            issued["n"] += 1
        return {nm: (wslot[i % nslot], wB[i % nslot]) for nm, i in zip(names, idxs)}

    def rstd_from_ss(ss_ap, ssB):
        r, rB = RS.get()
        ts("pool", r[:, 0:1], ss_ap, 1.0 / D, EPS, ALU.mult, ALU.add, [ssB], [rB])
        tt("pool", r[:, 1:2], r[:, 0:1], chalf[:, 0:1], ALU.pow, [rB, constB], [rB])
        return r[:, 1:2], rB

    def norm_c1(slot, np_):
        junk, junkB = acc_flat[:, slot * 1024:(slot + 1) * 1024], junkQB[slot]
        ssT, ssB = RS.get()
        act(junk[0:np_, :], x[0:np_, slot, :], AF.Square, [xB[slot]], [junkB, ssB], accum_out=ssT[0:np_, 0:1])
        return rstd_from_ss_p(ssT, ssB, np_)

    def norm_c1_keep(slot, np_):
        junk, junkB = RB1024.get()
        ssT, ssB = RS.get()
        act(junk[0:np_, :], x[0:np_, slot, :], AF.Square, [xB[slot]], [junkB, ssB], accum_out=ssT[0:np_, 0:1])
        r, rB = RS.get()
        ts("pool", r[0:np_, 0:1], ssT[0:np_, 0:1], 1.0 / D, EPS, ALU.mult, ALU.add, [ssB], [rB])
        tt("pool", rs_keep[0:np_, slot:slot + 1], r[0:np_, 0:1], chalf[0:np_, 0:1], ALU.pow, [rB, constB], [rskB[slot]])

    def norm_c2(slot, np_, rs, rsB):
        htm, htmB = aT_flat[:, slot * 1024:(slot + 1) * 1024], htmQB[slot]
        act(htm[0:np_, :], x[0:np_, slot, :], AF.Identity, [xB[slot], rsB], [htmB], scale=rs[0:np_, :])
        return htm, htmB

    def norm_chain(slot, np_):
        rs, rsB = norm_c1(slot, np_)
        return norm_c2(slot, np_, rs, rsB)

    def norm_tr(htm, htmB, gi, col, np_, gT, l, bk=None):
        if bk is None:
            bk = bank()
        pbf = ps[:, bk, :].bitcast(BF16).rearrange("p (c t) -> p c t", c=8)
        for c in range(8):
            tr(pbf[:, c, 0:np_], htm[0:np_, c * 128:(c + 1) * 128], identb[0:np_, 0:np_],
               [htmB, constB], [pB[bk]], signal=(c == 7))
        tt("dve", hT[:, :, col:col + np_], pbf[:, :, 0:np_],
           gT[:, l, :].unsqueeze(2).to_broadcast([128, 8, np_]), ALU.mult,
           [pB[bk], constB], [hTB[gi]])

    NORM_DL = 3

    def norm_to_hT(groups, gT, l, kept=False):
        S.alias(aTB, htmQB)
        S.alias(accB, junkQB)
        blks = blocks_of(groups)
        nbk = len(blks)
        st = {}
        for i in range(nbk + 3):
            if i - 1 >= 0 and i - 1 < nbk:
                gi, g, slot, np_, col = blks[i - 1]
                st[i - 1] = norm_c2(slot, np_, *st[i - 1]) + (gi, col, np_, gT, l)
            if i < nbk:
                gi, g, slot, np_, col = blks[i]
                st[i] = (rs_keep[:, slot:slot + 1], rskB[slot]) if kept else norm_c1(slot, np_)
            if i - 3 >= 0 and i - 3 < nbk:
                norm_tr(*st.pop(i - 3))
        S.alias(htmQB, aTB)
        S.alias(junkQB, accB)

    def rstd_from_ss_p(ssT, ssB, np_):
        r, rB = RS.get()
        ts("pool", r[0:np_, 0:1], ssT[0:np_, 0:1], 1.0 / D, EPS, ALU.mult, ALU.add, [ssB], [rB])
        tt("pool", r[0:np_, 1:2], r[0:np_, 0:1], chalf[0:np_, 0:1], ALU.pow, [rB, constB], [rB])
        return r[:, 1:2], rB

    def proj_fm(wt, wBuf, j4, gi, g):
        bk = bank()
        for k in range(8):
            mm(ps[:, bk, 0:g.n], wt[:, k, j4 * 128:(j4 + 1) * 128], hT[:, k, g.c0:g.c0 + g.n],
               k == 0, k == 7, [wBuf, hTB[gi]], [pB[bk]], signal=(k == 7))
        return bk

    def gate_acc(l, gi, g, i, bkY, bkG, mode, scale_ap=None):
        n = g.n
        th, thB = R528.get()
        act(th[:, 0:n], ps[:, bkG, 0:n], AF.Tanh, [pB[bkG]], [thB], scale=0.5)
        dst = acc[:, i, g.c0:g.c0 + n]
        if mode == "set":
            stt(dst, th[:, 0:n], 1.0, ps[:, bkY, 0:n], ALU.add, ALU.mult, [thB, pB[bkY]], [accB[gi]])
        else:
            tg, tgB = R528.get()
            stt(tg[:, 0:n], th[:, 0:n], 1.0, ps[:, bkY, 0:n], ALU.add, ALU.mult, [thB, pB[bkY]], [tgB])
            if scale_ap is not None:
                stt(dst, tg[:, 0:n], scale_ap, dst, ALU.mult, ALU.add, [tgB, accB[gi], constB], [accB[gi]])
            else:
                tt("dve", dst, dst, tg[:, 0:n], ALU.add, [tgB, accB[gi]], [accB[gi]])

    def phase_conv(ti, l, groups):
        has_s = any(g.kind == "s" for g in groups)
        if has_s:
            prep_sample_conv(l)
        for h in range(2):
            W = need(ti, l, [("hc", h), ("cg", h), ("bg", h)])
            for gi, g in enumerate(groups):
                n = g.n
                for j4 in range(4):
                    j = h * 4 + j4
                    bA = proj_fm(*W[("hc", h)], j4, gi, g)
                    bC = proj_fm(*W[("cg", h)], j4, gi, g)
                    bG = proj_fm(*W[("bg", h)], j4, gi, g)
                    hcs, hcsB = R528.get()
                    cp("act", hcs[:, 0:n], ps[:, bA, 0:n], [pB[bA]], [hcsB])
                    ub, ubB = R528.get()
                    c1, c1B = R528.get()
                    if g.kind == "p":
                        cp("act", ub[:, 0:2], cstate[:, l, j, :], [cstB[l][j]], [ubB])
                        tt("dve", ub[:, 2:2 + n], ps[:, bC, 0:n], hcs[:, 0:n], ALU.mult, [pB[bC], hcsB], [ubB])
                        cp("act", cstate[:, l, j, :], ub[:, n:n + 2], [ubB], [cstB[l][j]])
                        act(c1[:, 0:n], ub[:, 0:n], AF.Identity, [ubB, constB], [c1B], scale=cwT[:, l, 0, j:j + 1])
                        stt(c1[:, 0:n], ub[:, 1:1 + n], cwT[:, l, 1, j:j + 1], c1[:, 0:n], ALU.mult, ALU.add,
                            [ubB, c1B, constB], [c1B])
                        stt(c1[:, 0:n], ub[:, 2:2 + n], cwT[:, l, 2, j:j + 1], c1[:, 0:n], ALU.mult, ALU.add,
                            [ubB, c1B, constB], [c1B])
                    else:
                        sample_conv(l, j, n, bC, hcs, hcsB, ub, ubB, c1, c1B)
                    tt("dve", aT[:, j, g.c0:g.c0 + n], ps[:, bG, 0:n], c1[:, 0:n], ALU.mult, [pB[bG], c1B], [aTB[gi]])
        if has_s:
            finish_sample_conv(l)
        for h in range(2):
            W = need(ti, l, [("wco", h), ("gc", h)])
            wco, wcoB = W[("wco", h)]
            for gi, g in enumerate(groups):
                n = g.n
                for i4 in range(4):
                    i = h * 4 + i4
                    bY = bank()
                    for jj in range(8):
                        mm(ps[:, bY, 0:n], wco[:, jj, i4 * 128:(i4 + 1) * 128], aT[:, jj, g.c0:g.c0 + n],
                           jj == 0, jj == 7, [wcoB, aTB[gi]], [pB[bY]], signal=(jj == 7))
                    bGt = proj_fm(*W[("gc", h)], i4, gi, g)
                    gate_acc(l, gi, g, i, bY, bGt, "set")

    def phase_pool(ti, l, groups):
        has_s = any(g.kind == "s" for g in groups)
        if has_s:
            free_slot = max(slot for g in groups for (slot, cb) in g.blocks) + 1
            assert free_slot < NBS
            prep_sample_pool(l, free_slot)
        for h in range(2):
            W = need(ti, l, [("up", h)])
            for gi, g in enumerate(groups):
                n = g.n
                for j4 in range(4):
                    j = h * 4 + j4
                    pg = j // 2
                    win = POOL_WINS[pg]
                    bU = proj_fm(*W[("up", h)], j4, gi, g)
                    if g.kind == "s":
                        sample_pool(l, j, n, bU, gi, g)
                        continue
                    P, PB = R528.get()
                    cp("act", P[:, 0:15], pstate[:, l, j, :], [pstB[l][j]], [PB])
                    cp("act", P[:, 15:15 + n], ps[:, bU, 0:n], [pB[bU]], [PB])
                    cp("act", pstate[:, l, j, :], P[:, n:n + 15], [PB], [pstB[l][j]])
                    W_ = 15 + n
                    src, srcB = P, PB
                    sh = 1
                    while sh < win:
                        dst, dstB = R528.get()
                        tt("dve", dst[:, sh:W_], src[:, sh:W_], src[:, 0:W_ - sh], ALU.add, [srcB], [dstB])
                        src, srcB = dst, dstB
                        sh *= 2
                    stt(aT[:, j, g.c0:g.c0 + n], src[:, 15:15 + n], 1.0 / win, P[:, 15:15 + n], ALU.mult, ALU.subtract,
                        [srcB, PB], [aTB[gi]])
                    if g.first_chain:
                        tf, tfB = R528.get()
                        tt("dve", tf[:, 0:16], src[:, 15:31], invcnt[:, j, :], ALU.mult, [srcB, constB], [tfB])
                        tt("dve", aT[:, j, g.c0:g.c0 + 16], tf[:, 0:16], P[:, 15:31], ALU.subtract, [tfB, PB], [aTB[gi]])
        if has_s:
            finish_sample_pool(l)
        for h in range(2):
            names = [("wpool", 0), ("gp", h)]
            W = need(ti, l, names)
            wp, wpB = W[("wpool", 0)]
            for gi, g in enumerate(groups):
                n = g.n
                for i4 in range(4):
                    i = h * 4 + i4
                    pg, oc = i // 2, i % 2
                    bY = bank()
                    for kc in range(2):
                        mm(ps[:, bY, 0:n], wp[:, pg * 2 + kc, oc * 128:(oc + 1) * 128],
                           aT[:, pg * 2 + kc, g.c0:g.c0 + n], kc == 0, kc == 1, [wpB, aTB[gi]], [pB[bY]],
                           signal=(kc == 1))
                    bGt = proj_fm(*W[("gp", h)], i4, gi, g)
                    gate_acc(l, gi, g, i, bY, bGt, "add", scale_ap=pscT[:, l, i:i + 1])

    def rope_pre(bk, n):
        qb, qbB = RB512.get()
        cp("act", qb[:, 0:n], ps[:, bk, 0:n], [pB[bk]], [qbB])
        return qb, qbB

    def rope_post(g, bk, qb, qbB, dst_ap, dstBufs, n, extra_f32=None, after=None):
        bR = bank()
        mm(ps[:, bR, 0:n], rotb[:, :], qb[:, 0:n], True, True, [qbB, constB], [pB[bR]], signal=True)
        t1, t1B = R528.get()
        t2, t2B = R528.get()
        if g.kind == "p":
            tt("dve", t1[:, 0:n], ps[:, bk, 0:n], ropeT[:, 0, 0:n], ALU.mult, [pB[bk], ropeB], [t1B])
            tt("dve", t2[:, 0:n], ps[:, bR, 0:n], ropeT[:, 1, 0:n], ALU.mult, [pB[bR], ropeB], [t2B])
        else:
            ts("dve", t1[:, 0:n], ps[:, bk, 0:n], ropeT[:, 0, 0:1], None, ALU.mult, None, [pB[bk], ropeB], [t1B])
            ts("dve", t2[:, 0:n], ps[:, bR, 0:n], ropeT[:, 1, 0:1], None, ALU.mult, None, [pB[bR], ropeB], [t2B])
        if extra_f32 is not None:
            ef, efB = extra_f32
            tt("dve", ef, t1[:, 0:n], t2[:, 0:n], ALU.add, [t1B, t2B], [efB])
            cp("act", dst_ap, ef, [efB], dstBufs)
        else:
            tt("dve", dst_ap, t1[:, 0:n], t2[:, 0:n], ALU.add, [t1B, t2B], dstBufs)
        if after is not None:
            after()

    def rope_chunk(l, g, bk, dst_ap, dstBufs, n, extra_f32=None):
        qb, qbB = rope_pre(bk, n)
        rope_post(g, bk, qb, qbB, dst_ap, dstBufs, n, extra_f32=extra_f32)

    def load_rope(g):
        if g.kind == "p":
            cb0 = g.blocks[0][1]
            S.dma("sp", ropesem, ropeT[:, 0, 0:g.n], rope_cos[:, cb0 * 128:cb0 * 128 + g.n], writes=[ropeB])
            S.dma("sp", ropesem, ropeT[:, 1, 0:g.n], rope_sin[:, cb0 * 128:cb0 * 128 + g.n], writes=[ropeB])
        else:
            S.dma("sp", ropesem, ropeT[:, 0, 0:1], rope_s[:, 0:1], writes=[ropeB], allow_slow_non_contiguous=True)
            S.dma("sp", ropesem, ropeT[:, 1, 0:1], rope_s[:, 1:2], writes=[ropeB], allow_slow_non_contiguous=True)

    def phase_attn(ti, l, groups, is_last_tile):
        Wq = need(ti, l, [("q", 0), ("q", 1), ("kv", 0), ("kdup", 0)])
        kd, kdB = Wq[("kdup", 0)]
        wkv, wvB = Wq[("kv", 0)]
        wv = wkv[:, :, 256:512]
        kdv = kd[:, :, :].rearrange("p kc (h r e) -> p kc h r e", h=4, r=2)
        for r in range(2):
            cp("dve", kdv[:, :, :, r, :], wkv[:, :, 0:256].rearrange("p kc (h e) -> p kc h e", h=4), [wvB], [kdB])
        for gi, g in enumerate(groups):
            n = g.n
            load_rope(g)
            pend = None
            for c in range(8):
                bq = proj_fm(*Wq[("q", c // 4)], c % 4, gi, g)
                qb, qbB = rope_pre(bq, n)
                if pend is not None:
                    rope_post(*pend)
                pend = (g, bq, qb, qbB, QT[:, c, g.c0:g.c0 + n], [QTB[gi]], n)
            if g.kind == "s":
                rope_post(*pend)
                pend = None
                sample_attn(ti, l, gi, g, kd, kdB, wv, wvB)
                continue
            last_chain = is_last_tile and gi == len([gg for gg in groups if gg.kind == "p"]) - 1
            for hh in range(4):
                bkk = proj_fm(kd, kdB, hh, gi, g)
                qb, qbB = rope_pre(bkk, n)
                if pend is not None:
                    rope_post(*pend)
                if last_chain:
                    nb_ = len(g.blocks)
                    t3, t3B = R528.get()

                    def fin(hh=hh, t3=t3, t3B=t3B, nb_=nb_):
                        cp("act", kfin[:, hh, :], t3[:, (nb_ - 1) * 128:nb_ * 128], [t3B], [kfinB])
                    pend = (g, bkk, qb, qbB, KTbuf[l][:, hh, 128:128 + n], [KTB[l]], n, (t3[:, 0:n], t3B), fin)
                else:
                    pend = (g, bkk, qb, qbB, KTbuf[l][:, hh, 128:128 + n], [KTB[l]], n)
            rope_post(*pend)
            pend = None
            for bi, (slot, cb) in enumerate(g.blocks):
                bv = bank()
                col = g.c0 + bi * 128
                for k in range(8):
                    mm(ps[:, bv, 0:256], hT[:, k, col:col + 128], wv[:, k, 0:256], k == 0, k == 7,
                       [wvB, hTB[gi]], [pB[bv]], signal=(k == 7))
                cp("act", Vbuf[l][:, 1 + bi, :, 0:64], ps[:, bv, 0:256].rearrange("p (h e) -> p h e", h=4),
                   [pB[bv]], [VB[l][1 + bi]])
                if last_chain and bi == len(g.blocks) - 1:
                    cp("act", vfin[:, :], ps[:, bv, 0:256], [pB[bv]], [vfinB])
            pstate_rr["avoid"] = (4, 5, 6, 7)
            bO = [4, 5, 6, 7]
            pendT = None

            def emit_T(On, OnB, col, gi=gi):
                bT = bank()
                pbf = ps[:, bT, :].bitcast(BF16).rearrange("p (c t) -> p c t", c=8)
                for c in range(8):
                    tr(pbf[:, c, :], On[:, c * 128:(c + 1) * 128], identb[:, :], [OnB, constB], [pB[bT]],
                       signal=(c == 7))
                cp("act", aT[:, :, col:col + 128], pbf[:, :, :], [pB[bT]], [aTB[gi]])
            for bi, (slot, cb) in enumerate(g.blocks):
                col = g.c0 + bi * 128
                has_prev = not (g.first_chain and bi == 0)
                whs = (["prev", "cur"] if has_prev else ["cur"])
                PTs = {}
                for hp in range(2):
                    if hp == 1 and pendT is not None:
                        emit_T(*pendT)
                        pendT = None
                    for which in whs:
                        kc0 = bi * 128 if which == "prev" else 128 + bi * 128
                        bSr = [bank(), bank()]
                        for pi in range(2):
                            hh = 2 * hp + pi
                            for r in range(2):
                                mm(ps[:, bSr[r], pi * 256:(pi + 1) * 256].rearrange("p (a b) -> p a b", a=2),
                                   KTbuf[l][r * 64:(r + 1) * 64, hh, kc0:kc0 + 128],
                                   QT[r * 64:(r + 1) * 64, 2 * hh:2 * hh + 2, col:col + 128],
                                   True, True, [KTB[l], QTB[gi]], [pB[bSr[r]]], signal=(pi == 1))
                        for r in range(2):
                            PT, PTB = RB512.get()
                            act(PT[:, :], ps[:, bSr[r], :], AF.Exp, [pB[bSr[r]]], [PTB], scale=0.125)
                            tt("dve", PT[:, :], PT[:, :], (maskp if which == "prev" else maskc)[:, :], ALU.mult,
                               [PTB, constB], [PTB])
                            PTs[(r, hp, which)] = (PT, PTB)
                for r in range(2):
                    for hp in range(2):
                        hlist = [(2 * hp, r), (2 * hp, r + 2), (2 * hp + 1, r), (2 * hp + 1, r + 2)]
                        for s_i, (hh, gq) in enumerate(hlist):
                            for wi, which in enumerate(whs):
                                PT, PTB = PTs[(r, hp, which)]
                                vslot = bi if which == "prev" else 1 + bi
                                mm(ps[:, bO[hh], gq * 65:gq * 65 + 65], PT[:, s_i * 128:(s_i + 1) * 128],
                                   Vbuf[l][:, vslot, hh, :], wi == 0, wi == len(whs) - 1,
                                   [PTB, VB[l][vslot]], [pB[bO[hh]]], signal=(wi == len(whs) - 1))
                pso = ps[:, 4:8, 0:260].rearrange("p b (g e) -> p b g e", e=65)
                rd, rdB = RS16.get()
                rdv = rd[:, 0:16].rearrange("p (b g) -> p b g", b=4)
                tt("dve", rdv.unsqueeze(3), pso[:, :, :, 64:65],
                   sinkexp[:, l, :].rearrange("p (b g) -> p b g", b=4).unsqueeze(3), ALU.add,
                   [pB[b_] for b_ in bO] + [constB], [rdB])
                rr, rrB = RS16.get()
                S.op("dve", lambda e: e.reciprocal(out=rr[:, 0:16], in_=rd[:, 0:16]), reads=[rdB], writes=[rrB])
                On, OnB = RB1024.get()
                tt("dve", On[:, :].rearrange("p (b g e) -> p b g e", b=4, g=4), pso[:, :, :, 0:64],
                   rr[:, 0:16].rearrange("p (b g) -> p b g", b=4).unsqueeze(3).to_broadcast([128, 4, 4, 64]),
                   ALU.mult, [pB[b_] for b_ in bO] + [rrB], [OnB])
                pendT = (On, OnB, col)
            emit_T(*pendT)
            pendT = None
            pstate_rr["avoid"] = ()
            nb_ = len(g.blocks)
            cp("act", KTbuf[l][:, :, 0:128], KTbuf[l][:, :, nb_ * 128:(nb_ + 1) * 128], [KTB[l]], [KTB[l]])
            cp("act", Vbuf[l][:, 0, :, 0:64], Vbuf[l][:, nb_, :, 0:64], [VB[l][nb_]], [VB[l][0]])
        for h in range(2):
            W = need(ti, l, [("wao", h), ("ga", h)])
            wao, waoB = W[("wao", h)]
            for gi, g in enumerate(groups):
                n = g.n
                for i4 in range(4):
                    i = h * 4 + i4
                    bY = bank()
                    for jj in range(8):
                        mm(ps[:, bY, 0:n], wao[:, jj, i4 * 128:(i4 + 1) * 128], aT[:, jj, g.c0:g.c0 + n],
                           jj == 0, jj == 7, [waoB, aTB[gi]], [pB[bY]], signal=(jj == 7))
                    bGt = proj_fm(*W[("ga", h)], i4, gi, g)
                    gate_acc(l, gi, g, i, bY, bGt, "add")
                    act(QT[:, i, g.c0:g.c0 + n], acc[:, i, g.c0:g.c0 + n], AF.Copy, [accB[gi]], [QTB[gi]], scale=0.5)

    def post_norm_update(slot, np_, halves, gidx, merged=None):
        tm, tmB = RF1024.get()
        if merged is not None:
            ap, bufs = merged
            junk, junkB = RB1024.get()
            sst, sstB = RS.get()
            act(junk[0:np_, :].rearrange("p (a b) -> p a b", a=2), ap, AF.Square, bufs, [junkB, sstB],
                accum_out=sst[0:np_, 0:1])
            rs, rsB = rstd_from_ss_p(sst, sstB, np_)
            stt(tm[0:np_, :].rearrange("p (a b) -> p a b", a=2), ap, rs[0:np_, :],
                gpost[0:np_, gidx, :].rearrange("p (a b) -> p a b", a=2), ALU.mult, ALU.mult,
                bufs + [rsB, gpostB], [tmB])
        else:
            sss = []
            for (ap, bufs) in halves:
                junk, junkB = RB512.get()
                ssT, ssB = RS.get()
                act(junk[0:np_, :], ap, AF.Square, bufs, [junkB, ssB], accum_out=ssT[0:np_, 0:1])
                sss.append((ssT, ssB))
            sst, sstB = RS.get()
            tt("pool", sst[0:np_, 0:1], sss[0][0][0:np_, 0:1], sss[1][0][0:np_, 0:1], ALU.add,
               [sss[0][1], sss[1][1]], [sstB])
            rs, rsB = rstd_from_ss_p(sst, sstB, np_)
            for hi, (ap, bufs) in enumerate(halves):
                stt(tm[0:np_, hi * 512:(hi + 1) * 512], ap, rs[0:np_, :], gpost[0:np_, gidx, hi * 512:(hi + 1) * 512],
                    ALU.mult, ALU.mult, bufs + [rsB, gpostB], [tmB])
        tt("dve", x[0:np_, slot, :], x[0:np_, slot, :], tm[0:np_, :], ALU.add, [xB[slot], tmB], [xB[slot]])

    def blocks_of(groups):
        out = []
        for gi, g in enumerate(groups):
            for bi, (slot, cb) in enumerate(g.blocks):
                np_ = 128 if g.kind == "p" else NS
                col = g.c0 + bi * 128
                out.append((gi, g, slot, np_, col))
        return out

    def pn_b1(np_, ap, bufs):
        junk, junkB = RB1024.get()
        sst, sstB = RS.get()
        act(junk[0:np_, :].rearrange("p (a b) -> p a b", a=2), ap, AF.Square, bufs, [junkB, sstB],
            accum_out=sst[0:np_, 0:1])
        return rstd_from_ss_p(sst, sstB, np_)

    def pn_b2(slot, np_, ap, bufs, rs, rsB, gidx):
        tm, tmB = RF1024.get()
        stt(tm[0:np_, :].rearrange("p (a b) -> p a b", a=2), ap, rs[0:np_, :],
            gpost[0:np_, gidx, :].rearrange("p (a b) -> p a b", a=2), ALU.mult, ALU.mult,
            bufs + [rsB, gpostB], [tmB])
        tt("dve", x[0:np_, slot, :], x[0:np_, slot, :], tm[0:np_, :], ALU.add, [xB[slot], tmB], [xB[slot]])

    def phase_mix(ti, l, groups):
        W = need(ti, l, [("wmix", 0), ("wmix", 1)])
        S.alias(aTB, htmQB)
        S.alias(accB, junkQB)
        blks = blocks_of(groups)
        order = sorted(range(len(groups)), key=lambda gi_: (groups[gi_].kind == "s", -gi_))
        blks = [b_ for gi_ in order for b_ in blks if b_[0] == gi_]
        nbk = len(blks)
        A = {}
        B1 = {}
        C1 = {}
        C2 = {}
        pend = []
        for i in range(nbk + 5):
            if i < nbk:
                gi, g, slot, np_, col = blks[i]
                b0 = 2 * (i % 3)
                for cgi in range(2):
                    wm, wmB = W[("wmix", cgi)]
                    for c in range(8):
                        mm(ps[0:np_, b0 + cgi, :], QT[:, c, col:col + np_], wm[:, c, :], c == 0, c == 7,
                           [wmB, QTB[gi]], [pB[b0 + cgi]], signal=(c == 7))
                A[i] = (ps[0:np_, b0:b0 + 2, :], [pB[b0], pB[b0 + 1]])
            j = i - 5
            if 0 <= j < nbk:
                pend.append(C2.pop(j))
                if i < nbk:
                    norm_tr(*pend.pop(0), bk=6 + (i % 2))
            j = i - 4
            if 0 <= j < nbk:
                gi, g, slot, np_, col = blks[j]
                C2[j] = norm_c2(slot, np_, *C1.pop(j)) + (gi, col, np_, gmlpT, l)
            j = i - 3
            if 0 <= j < nbk:
                gi, g, slot, np_, col = blks[j]
                C1[j] = norm_c1(slot, np_)
            j = i - 1
            if 0 <= j < nbk:
                gi, g, slot, np_, col = blks[j]
                B1[j] = pn_b1(np_, *A[j])
            j = i - 2
            if 0 <= j < nbk:
                gi, g, slot, np_, col = blks[j]
                pn_b2(slot, np_, *A.pop(j), *B1.pop(j), 0)
        return pend

    def phase_mlp(ti, l, groups, pendq=(), final_hook=None, keep_stats=False):
        pendq = list(pendq)
        S.alias(accB + aTB + QTB + htmQB + junkQB, FTB)
        def up_chunks(i, wu, wuB, gi, g):
            n = g.n
            for f4 in range(4):
                bk = proj_fm(wu, wuB, f4, gi, g)
                rl, rlB = RB512.get()
                act(rl[:, 0:n], ps[:, bk, 0:n], AF.Relu, [pB[bk]], [rlB])
                tt("dve", FT[:, i * 4 + f4, g.c0:g.c0 + n], rl[:, 0:n], rl[:, 0:n], ALU.mult, [rlB], [FTB[gi]])

        def flush_pend(gi):
            for e_ in [e_ for e_ in pendq if e_[2] == gi]:
                pendq.remove(e_)
                norm_tr(*e_)
                if not pendq:
                    S.alias(htmQB, FTB)

        NF = 3
        W = need(ti, l, [("wup", i) for i in range(NF)])
        order = sorted(range(len(groups)), key=lambda gi_: (groups[gi_].kind == "s", -gi_))
        for gi in order:
            g = groups[gi]
            flush_pend(gi)
            for i in range(NF):
                up_chunks(i, *W[("wup", i)], gi, g)
        assert not pendq
        for i in range(NF, 8):
            W = need(ti, l, [("wup", i)])
            for gi, g in enumerate(groups):
                up_chunks(i, *W[("wup", i)], gi, g)
        stash = hT[:, :, :].rearrange("p c t -> p (c t)")[:, 0:(8 * TT // 1024) * 1024].bitcast(F32).rearrange("p (b c) -> p b c", c=512)
        stB = [Buf(f"stash{b}") for b in range(NBS)]
        S.alias(hTB, stB)
        blks = blocks_of(groups)
        assert len(blks) <= 8
        for kq in range(4):
            W = need(ti, l, [("wdn", (0, kq))])
            wd, wdB = W[("wdn", (0, kq))]
            for bi_, (gi, g, slot, np_, col) in enumerate(blks):
                for f8 in range(8):
                    f = kq * 8 + f8
                    mm(ps[0:np_, bi_, :], FT[:, f, col:col + np_], wd[:, f8, :], f == 0, f == 31,
                       [wdB, FTB[gi]], [pB[bi_]], signal=(f8 == 7))
        for bi_, (gi, g, slot, np_, col) in enumerate(blks):
            cp("act", stash[0:np_, slot, :], ps[0:np_, bi_, :], [pB[bi_]], [stB[slot]])
        pstate_rr["i"] = 0
        W = need(ti, l, [("wdn", (1, kq)) for kq in range(4)])
        for (gi, g, slot, np_, col) in blks:
            bk = bank()
            for f in range(32):
                wd, wdB = W[("wdn", (1, f // 8))]
                mm(ps[0:np_, bk, :], FT[:, f, col:col + np_], wd[:, f % 8, :], f == 0, f == 31,
                   [wdB, FTB[gi]], [pB[bk]], signal=(f == 31))
            post_norm_update(slot, np_, [(stash[0:np_, slot, :], [stB[slot]]), (ps[0:np_, bk, :], [pB[bk]])], 1)
            if keep_stats:
                norm_c1_keep(slot, np_)
            if final_hook is not None:
                final_hook(g, slot)
        S.alias(stB, hTB)
        S.alias(FTB, accB + aTB + QTB + htmQB + junkQB)

    def tm_rows_out(srcT, srcB, dst_ap, sem):
        bk = bank_pair()
        for j in range(8):
            o = ps[0:NS, bk, j * 128:(j + 1) * 128] if j < 4 else ps[0:NS, bk + 1, (j - 4) * 128:(j - 3) * 128]
            tr(o, srcT[:, j, :], identf[:, :], [srcB, constB], [pB[bk], pB[bk + 1]], signal=True)
        o1, o1B = RF1024.get()
        cp("dve", o1[0:NS, 0:512], ps[0:NS, bk, :], [pB[bk]], [o1B])
        cp("dve", o1[0:NS, 512:1024], ps[0:NS, bk + 1, :], [pB[bk + 1]], [o1B])
        final_toks.append(S.dma("sp", sem, dst_ap, o1[0:NS, :], reads=[o1B]))

    def prep_sample_conv(l):
        st, stB_ = RF1024.get()
        S.dma("sp", ssem[0], st[0:32, :], sconv[l].rearrange("b t d -> (b t) d"), writes=[stB_])
        bk = bank()
        for j in range(8):
            tr(ps[:, bk, j * 32:(j + 1) * 32], st[0:32, j * 128:(j + 1) * 128], identf[0:32, 0:32],
               [stB_, constB], [pB[bk]], signal=(j == 7))
        cp("dve", scst[:, :, :].rearrange("p a b -> p (a b)"), ps[:, bk, 0:256], [pB[bk]], [scstB])

    def sample_conv(l, j, n, bC, hcs, hcsB, ub, ubB, c1, c1B):
        tt("dve", ub[:, 0:n], ps[:, bC, 0:n], hcs[:, 0:n], ALU.mult, [pB[bC], hcsB], [ubB])
        cp("act", unew[:, j, :], ub[:, 0:n], [ubB], [unewB])
        sv_ = scst[:, j, :].rearrange("p (b t) -> p b t", t=2)
        ts("dve", c1[:, 0:n], sv_[:, :, 0], cwT[:, l, 0, j:j + 1], None, ALU.mult, None, [scstB, constB], [c1B])
        stt(c1[:, 0:n], sv_[:, :, 1], cwT[:, l, 1, j:j + 1], c1[:, 0:n], ALU.mult, ALU.add, [scstB, c1B, constB], [c1B])
        stt(c1[:, 0:n], ub[:, 0:n], cwT[:, l, 2, j:j + 1], c1[:, 0:n], ALU.mult, ALU.add, [ubB, c1B, constB], [c1B])

    def finish_sample_conv(l):
        tm_rows_out(unew, unewB, sconv_o[l, :, 1, :], ssem[1])
        final_toks.append(S.dma("sp", ssem[2], sconv_o[l, :, 0, :], sconv[l, :, 1, :]))

    sp_tiles = {}

    def prep_sample_pool(l, free_slot):
        spA, spAB = RF1024.get()
        rows = spool[l].rearrange("b t d -> (b t) d")
        S.dma("sp", ssem[0], spA[0:120, :], rows[0:120, :], writes=[spAB])
        S.dma("sp", ssem[3], x[0:120, free_slot, :], rows[120:240, :], writes=[xB[free_slot]])
        sp_tiles["A"] = (spA, spAB)
        sp_tiles["B"] = (x[:, free_slot, :], xB[free_slot])

    def sample_pool(l, j, n, bU, gi, g):
        cp("act", upnew[:, j, :], ps[:, bU, 0:n], [pB[bU]], [upnewB])
        bk = bank()
        for hlf, key in enumerate(("A", "B")):
            tl, tlB = sp_tiles[key]
            tr(ps[:, bk, hlf * 120:(hlf + 1) * 120], tl[0:120, j * 128:(j + 1) * 128], identf[0:120, 0:120],
               [tlB, constB], [pB[bk]], signal=(hlf == 1))
        stv = ps[:, bk, 0:240].rearrange("p (b t) -> p b t", t=15)
        win = POOL_WINS[j // 2]
        red, redB = R528.get()
        S.op("dve", lambda e: e.tensor_reduce(out=red[:, 0:NS], in_=stv[:, :, 15 - (win - 1):15], axis=AX.X, op=ALU.add),
             reads=[pB[bk]], writes=[redB])
        tt("dve", red[:, 0:NS], red[:, 0:NS], upnew[:, j, :], ALU.add, [redB, upnewB], [redB])
        stt(aT[:, j, g.c0:g.c0 + NS], red[:, 0:NS], 1.0 / win, upnew[:, j, :], ALU.mult, ALU.subtract,
            [redB, upnewB], [aTB[gi]])

    def finish_sample_pool(l):
        tm_rows_out(upnew, upnewB, spool_o[l, :, 14, :], ssem[1])
        final_toks.append(S.dma("sp", ssem[2], spool_o[l, :, 0:14, :], spool[l, :, 1:15, :]))

    def sample_attn(ti, l, gi, g, kd, kdB, wv, wvB):
        n = NS
        c0 = g.c0
        for hh in range(4):
            bkk = proj_fm(kd, kdB, hh, gi, g)
            t3, t3B = R528.get()
            jb, jbB = RB512.get()
            rope_chunk(l, g, bkk, jb[:, 0:n], [jbB], n, extra_f32=(t3[:, 0:n], t3B))
            cp("act", kfs[:, hh, :], t3[:, 0:n], [t3B], [kfsB])
        bk = bank()
        for hh in range(4):
            tr(ps[0:NS, bk, hh * 128:(hh + 1) * 128], kfs[:, hh, :], identf[:, :], [kfsB, constB], [pB[bk]], signal=True)
        knew, knewB = R528.get()
        cp("dve", knew[0:NS, 0:256].rearrange("p (h e) -> p h e", h=4),
           ps[0:NS, bk, :].rearrange("p (h e) -> p h e", h=4)[:, :, 0:64], [pB[bk]], [knewB])
        S.dma("sp", ssem[4], sk_o[l, :, 127, :], knew[0:NS, 0:256], reads=[knewB], writes=[skoB[l]])
        bv = bank()
        for k in range(8):
            mm(ps[0:NS, bv, 0:256], hT[:, k, c0:c0 + NS], wv[:, k, 0:256], k == 0, k == 7,
               [wvB, hTB[gi]], [pB[bv]], signal=(k == 7))
        vnew, vnewB = R528.get()
        cp("act", vnew[0:NS, 0:256], ps[0:NS, bv, 0:256], [pB[bv]], [vnewB])
        S.dma("sp", ssem[5], sv_o[l, :, 127, :], vnew[0:NS, 0:256], reads=[vnewB], writes=[svoB[l]])
        bOT = bank()
        pstate_rr["avoid"] = (bOT,)
        def stage1(b):
            ks, ksB = R528.get()
            S.dma("sp", ssem[6 + b % 2], ks[:, 0:256], sk_o[l, b, :, :], reads=[skoB[l]], writes=[ksB])
            vs, vsB = R528.get()
            S.dma("sp", ssem[8 + b % 2], vs[:, 0:256], sv_o[l, b, :, :], reads=[svoB[l]], writes=[vsB])
            kb, kbB = RB512.get()
            for r in range(2):
                cp("dve", kb[:, :].rearrange("p (h r e) -> p h r e", h=4, r=2)[:, :, r, :],
                   ks[:, 0:256].rearrange("p (h e) -> p h e", h=4), [ksB], [kbB])
            vb, vbB = RB512.get()
            cp("act", vb[:, 0:256], vs[:, 0:256], [vsB], [vbB])
            bt = bank()
            pbf = ps[:, bt, :].bitcast(BF16)
            for hh in range(4):
                tr(pbf[:, hh * 128:(hh + 1) * 128], kb[:, hh * 128:(hh + 1) * 128], identb[:, :], [kbB, constB],
                   [pB[bt]], signal=(hh == 3))
            kts, ktsB = RB512.get()
            cp("act", kts[:, :], pbf[:, 0:512], [pB[bt]], [ktsB])
            return vb, vbB, kts, ktsB

        def stage2(b, vb, vbB, kts, ktsB):
            pts = []
            for r in range(2):
                bS = bank()
                for hh in range(4):
                    mm(ps[:, bS, 2 * hh:2 * hh + 2], kts[r * 64:(r + 1) * 64, hh * 128:(hh + 1) * 128],
                       QT[r * 64:(r + 1) * 64, 2 * hh:2 * hh + 2, c0 + b:c0 + b + 1].rearrange("p a b -> p (a b)"),
                       True, True, [ktsB, QTB[gi]], [pB[bS]], signal=(hh == 3))
                pt, ptB = RB512.get()
                act(pt[:, 0:8], ps[:, bS, 0:8], AF.Exp, [pB[bS]], [ptB], scale=0.125)
                pts.append((pt, ptB))
            for r in range(2):
                pt, ptB = pts[r]
                ov = ps[r * 64:(r + 1) * 64, bOT, 0:128].rearrange("p (c s) -> p c s", s=NS)
                dv = ps[r * 64:(r + 1) * 64, bOT, 128:256].rearrange("p (c s) -> p c s", s=NS)
                for hh in range(4):
                    mm(ov[:, 2 * hh:2 * hh + 2, b], vb[:, hh * 64:(hh + 1) * 64], pt[:, 2 * hh:2 * hh + 2],
                       True, True, [vbB, ptB], [pB[bOT]], signal=False)
                mm(dv[:, :, b], ones64[:, :], pt[:, 0:8], True, True, [constB, ptB], [pB[bOT]], signal=True)

        cur = stage1(0)
        for b in range(NS):
            nxt = stage1(b + 1) if b + 1 < NS else None
            stage2(b, *cur)
            cur = nxt
        pstate_rr["avoid"] = ()
        rd, rdB = R528.get()
        tt("dve", rd[:, 0:128].rearrange("p (c b) -> p c b", c=8), ps[:, bOT, 128:256].rearrange("p (c b) -> p c b", c=8),
           sinkT[:, l, :].unsqueeze(2).to_broadcast([128, 8, NS]), ALU.add, [pB[bOT], constB], [rdB])
        rr, rrB = R528.get()
        S.op("dve", lambda e: e.reciprocal(out=rr[:, 0:128], in_=rd[:, 0:128]), reads=[rdB], writes=[rrB])
        tt("dve", aT[:, :, c0:c0 + NS], ps[:, bOT, 0:128].rearrange("p (c b) -> p c b", c=8),
           rr[:, 0:128].rearrange("p (c b) -> p c b", c=8), ALU.mult, [pB[bOT], rrB], [aTB[gi]])

    def emit_chain_outputs(l):
        bk = bank_pair()
        for j in range(8):
            tr(ps[0:2, bk, j * 128:(j + 1) * 128] if j < 4 else ps[0:2, bk + 1, (j - 4) * 128:(j - 3) * 128],
               cstate[:, l, j, :], identf[:, :], [cstB[l][j], constB], [pB[bk], pB[bk + 1]], signal=True)
        o1, o1B = RF1024.get()
        cp("dve", o1[0:2, 0:512], ps[0:2, bk, :], [pB[bk]], [o1B])
        cp("dve", o1[0:2, 512:1024], ps[0:2, bk + 1, :], [pB[bk + 1]], [o1B])
        final_toks.append(S.dma("sp", osem[0], pconv_o[l, :, :], o1[0:2, :], reads=[o1B]))
        bk = bank_pair()
        for j in range(8):
            tr(ps[0:15, bk, j * 128:(j + 1) * 128] if j < 4 else ps[0:15, bk + 1, (j - 4) * 128:(j - 3) * 128],
               pstate[:, l, j, :], identf[:, :], [pstB[l][j], constB], [pB[bk], pB[bk + 1]], signal=True)
        o2, o2B = RF1024.get()
        cp("dve", o2[0:15, 0:512], ps[0:15, bk, :], [pB[bk]], [o2B])
        cp("dve", o2[0:15, 512:1024], ps[0:15, bk + 1, :], [pB[bk + 1]], [o2B])
        final_toks.append(S.dma("sp", osem[1], ppool_o[l, :, :], o2[0:15, :], reads=[o2B]))
        bk = bank()
        for hh in range(4):
            tr(ps[:, bk, hh * 128:(hh + 1) * 128], kfin[:, hh, :], identf[:, :], [kfinB, constB], [pB[bk]], signal=True)
        o3, o3B = RF1024.get()
        cp("dve", o3[:, 0:256].rearrange("p (h e) -> p h e", h=4),
           ps[:, bk, :].rearrange("p (h e) -> p h e", h=4)[:, :, 0:64], [pB[bk]], [o3B])
        final_toks.append(S.dma("sp", osem[2], pk_o[l, :, :], o3[:, 0:256], reads=[o3B]))
        final_toks.append(S.dma("sp", osem[3], pv_o[l, :, :], vfin[:, :], reads=[vfinB]))

    blk0 = 0
    ntiles = len(tiles)
    for ti, (nb, hs) in enumerate(tiles):
        groups = []
        b = 0
        ngp = (nb + 3) // 4
        sizes = [nb // ngp + (1 if i < nb % ngp else 0) for i in range(ngp)]
        for m in sizes:
            groups.append(Group("p", [(b + i, blk0 + b + i) for i in range(m)], b * 128, m * 128,
                                first_chain=(ti == 0 and b == 0)))
            b += m
        if hs and with_samples:
            groups.append(Group("s", [(nb, -1)], nb * 128, NS))
        assert len(groups) <= MAXG
        is_last_tile = (blk0 + nb == nblk)
        for b in range(nb):
            if (ti, b) not in preloaded:
                S.dma("sp", xsem[b], x[:, b, :], xin[(blk0 + b) * 128:(blk0 + b + 1) * 128, :], writes=[xB[b]])
        if hs and with_samples:
            S.dma("sp", xsem[nb], x[0:NS, nb, :], xs[:, :], writes=[xB[nb]])
        for l in range(L):
            S.dma("sp", gpsem, gpost[:, 0, :], g_post_mix[l:l + 1, :].partition_broadcast(128).rearrange("p a b -> p (a b)"),
                  writes=[gpostB])
            S.dma("sp", gpsem, gpost[:, 1, :], g_post_mlp[l:l + 1, :].partition_broadcast(128).rearrange("p a b -> p (a b)"),
                  writes=[gpostB])
            norm_to_hT(groups, gpreT, l, kept=(l > 0))
            phase_conv(ti, l, groups)
            phase_pool(ti, l, groups)
            phase_attn(ti, l, groups, is_last_tile)
            pq = phase_mix(ti, l, groups)
            hook = None
            if l == L - 1:
                def hook(g, slot, blk0=blk0, nb=nb, ti=ti):
                    if g.kind == "p":
                        final_toks.append(S.dma("sp", xsem[slot], y[(blk0 + slot) * 128:(blk0 + slot + 1) * 128, :],
                                                x[:, slot, :], reads=[xB[slot]]))
                        if ti + 1 < ntiles and slot < tiles[ti + 1][0]:
                            nb0 = blk0 + nb
                            S.dma("sp", xsem[slot], x[:, slot, :], xin[(nb0 + slot) * 128:(nb0 + slot + 1) * 128, :],
                                  writes=[xB[slot]])
                            preloaded.add((ti + 1, slot))
                    else:
                        final_toks.append(S.dma("sp", xsem[slot], ys[:, :], x[0:NS, slot, :], reads=[xB[slot]]))
            phase_mlp(ti, l, groups, pq, hook, keep_stats=(l < L - 1))
            if is_last_tile:
                emit_chain_outputs(l)
        blk0 += nb
    assert blk0 == nblk
    for t in final_toks:
        S._wait("sp", t, True)
    return nc, S


def host_consts(nblk, p0):
    half = 32
    inv = (10000.0 ** (-np.arange(half, dtype=np.float32) / np.float32(half))).astype(np.float32)
    pos = (p0 + np.arange(nblk * 128)).astype(np.float32)
    fidx = np.arange(128) % 32
    ang = pos[None, :] * inv[fidx][:, None]
    cos = np.cos(ang).astype(np.float32)
    sin = np.sin(ang).astype(np.float32)
    k = np.arange(128)[:, None]
    q = np.arange(128)[None, :]
    maskc = np.tile((k <= q).astype(np.float32), (1, 4))
    maskp = np.tile((k > q).astype(np.float32), (1, 4))
    rot = np.zeros((128, 128), np.float32)
    for m in range(128):
        if m % 64 < 32:
            rot[m + 32, m] = -1.0
        else:
            rot[m - 32, m] = 1.0
    ident = np.eye(128, dtype=np.float32)
    invcnt = np.zeros((128, 8, 16), np.float32)
    for j in range(8):
        win = POOL_WINS[j // 2]
        invcnt[:, j, :] = 1.0 / np.minimum(win, np.arange(16) + 1).astype(np.float32)[None, :]
    return {"rope_cos": cos, "rope_sin": sin, "c_maskc": maskc, "c_maskp": maskp, "c_rot": rot,
            "c_ident": ident, "c_invcnt": invcnt.reshape(128, 128)}


def sample_inputs(inp, c):
    sl = slice(c * NS, (c + 1) * NS)
    half = 32
    inv = (10000.0 ** (-np.arange(half, dtype=np.float32) / np.float32(half))).astype(np.float32)
    ang = np.float32(16384.0) * inv[np.arange(128) % 32]
    rs = np.stack([np.cos(ang), np.sin(ang)], axis=1).astype(np.float32)
    return {"xs": np.ascontiguousarray(inp["x_sample"][sl, 0, :]),
            "sconv": np.ascontiguousarray(inp["state_conv"][:, sl]),
            "spool": np.ascontiguousarray(inp["state_pool"][:, sl]),
            "ck": np.ascontiguousarray(inp["cache_k"][:, sl].reshape(2, NS, 128, 256)),
            "cv": np.ascontiguousarray(inp["cache_v"][:, sl].reshape(2, NS, 128, 256)),
            "rope_s": rs}


FULL_TILES = [(7, False), (7, False), (7, False), (7, False), (5, True)]
WEIGHT_KEYS = ["w_in", "conv_w", "w_conv_out", "w_pool", "pool_scale", "attn_sinks", "w_attn_out", "w_mix_out",
               "norm_pre_mix", "norm_post_mix", "norm_pre_mlp", "norm_post_mlp", "w_up", "w_down"]
_CACHE = {}


def kernel(**inputs):
    inp = {k: np.ascontiguousarray(np.asarray(v)) for k, v in inputs.items()}
    nblk = NBLK_FULL
    if "prog" not in _CACHE:
        _CACHE["prog"] = build_program(nblk, FULL_TILES, with_samples=True)
    nc, _ = _CACHE["prog"]
    xp = inp["x_prompt"]
    in_maps = []
    for c in range(NCORES):
        s, h = c // 2, c % 2
        b0 = 0 if h == 0 else 64 - nblk
        m = {"xin": np.ascontiguousarray(xp[s, b0 * 128:(b0 + nblk) * 128, :])}
        for k in WEIGHT_KEYS:
            m[k] = inp[k]
        m.update(host_consts(nblk, b0 * 128))
        m.update(sample_inputs(inp, c))
        in_maps.append(m)
    res = run_bass_kernel_spmd(nc, in_maps, core_ids=list(range(NCORES)))
    R = res.results
    B = xp.shape[0]
    y_prompt = np.zeros((B, SEQ, D), np.float32)
    pc = np.zeros((2, B, 2, D), np.float32)
    pp = np.zeros((2, B, 15, D), np.float32)
    pk = np.zeros((2, B, 128, 4, 64), np.float32)
    pv = np.zeros((2, B, 128, 4, 64), np.float32)
    for c in range(NCORES):
        s, h = c // 2, c % 2
        yc = R[c]["y"]
        if h == 0:
            y_prompt[s, 0:nblk * 128] = yc
        else:
            b0 = 64 - nblk
            y_prompt[s, nblk * 128:] = yc[(nblk - b0) * 128:]
            pc[:, s] = R[c]["pconv_o"]
            pp[:, s] = R[c]["ppool_o"]
            pk[:, s] = R[c]["pk_o"].reshape(2, 128, 4, 64)
            pv[:, s] = R[c]["pv_o"].reshape(2, 128, 4, 64)
    ys = np.zeros_like(inp["x_sample"])
    sc = np.zeros_like(inp["state_conv"])
    sp_ = np.zeros_like(inp["state_pool"])
    sk = np.zeros_like(inp["cache_k"])
    sv = np.zeros_like(inp["cache_v"])
    for c in range(NCORES):
        sl = slice(c * NS, (c + 1) * NS)
        ys[sl, 0, :] = R[c]["ys"]
        sc[:, sl] = R[c]["sconv_o"]
        sp_[:, sl] = R[c]["spool_o"]
        sk[:, sl] = R[c]["sk_o"].reshape(2, NS, 128, 4, 64)
        sv[:, sl] = R[c]["sv_o"].reshape(2, NS, 128, 4, 64)
    return (y_prompt, ys, pc, pp, pk, pv, sc, sp_, sk, sv)
```

```python
import numpy as np
import concourse.bass as bass
import concourse.mybir as mybir
from concourse.bass_utils import run_bass_kernel_spmd

F32 = mybir.dt.float32
BF16 = mybir.dt.bfloat16
AF = mybir.ActivationFunctionType
ALU = mybir.AluOpType
AX = mybir.AxisListType

D = 1024
DFF = 4096
DIN = 8704
NCORES = 8
NS = 16
SEQ = 8192
NBLK_FULL = 33
EPS = 1e-6
POOL_WINS = (2, 4, 8, 16)


class Buf:
    __slots__ = ("name", "w", "r", "excl")

    def __init__(self, name, excl=False):
        self.name = name
        self.w = None
        self.r = {}
        self.excl = excl


class DmaSem:
    def __init__(self, nc, name):
        self.h = nc.alloc_semaphore(name)
        self.key = name
        self.val = 0
        self.last = None


class Sched:
    LIMIT = 20000

    def __init__(self, nc):
        self.nc = nc
        self.eng = {"pe": nc.tensor, "act": nc.scalar, "dve": nc.vector, "pool": nc.gpsimd, "sp": nc.sync}
        self.sem = {}
        self.cnt = {}
        self.gen = {}
        for e in self.eng:
            self.gen[e] = 0
            self.sem[e] = nc.alloc_semaphore(f"s_{e}_0")
            self.cnt[e] = 0
        self.seen = {e: {} for e in self.eng}
        self.ninst = {e: 0 for e in self.eng}

    def _key(self, e):
        return f"{e}{self.gen[e]}"

    def _wait(self, e, tok, raw):
        if tok is None:
            return
        key, h, val, src = tok
        if src == e and e == "pe":
            return
        if self.seen[e].get(key, 0) >= val:
            return
        self.eng[e].wait_ge(h, val)
        self.seen[e][key] = val

    def _deps(self, e, reads, writes):
        for b in reads:
            self._wait(e, b.w, True)
            if b.excl:
                for t in b.r.values():
                    if t[3] != e:
                        self._wait(e, t, False)
        for b in writes:
            self._wait(e, b.w, True)
            for t in b.r.values():
                self._wait(e, t, False)

    @staticmethod
    def _addr(b, tok):
        old = b.r.get(tok[0])
        if old is None or old[2] < tok[2]:
            b.r[tok[0]] = tok

    def _record(self, tok, reads, writes):
        for b in reads:
            self._addr(b, tok)
        for b in writes:
            b.w = tok
            b.r = {}

    def op(self, e, fn, reads=(), writes=(), signal=True):
        self._deps(e, reads, writes)
        inst = fn(self.eng[e])
        self.ninst[e] += 1
        if signal:
            self.cnt[e] += 1
            inst.then_inc(self.sem[e], 1)
            tok = (self._key(e), self.sem[e], self.cnt[e], e)
            if self.cnt[e] >= self.LIMIT:
                self.gen[e] += 1
                self.sem[e] = self.nc.alloc_semaphore(f"s_{e}_{self.gen[e]}")
                self.cnt[e] = 0
        else:
            tok = (self._key(e), self.sem[e], self.cnt[e] + 1, e)
        self._record(tok, reads, writes)
        return tok

    def dma(self, q, dsem, out, in_, reads=(), writes=(), **kw):
        self._deps(q, reads, writes)
        if dsem.last is not None:
            self._wait(q, dsem.last, True)
        inst = self.eng[q].dma_start(out=out, in_=in_, **kw)
        dsem.val += 16
        inst.then_inc(dsem.h, 16)
        tok = (dsem.key, dsem.h, dsem.val, None)
        dsem.last = tok
        self._record(tok, reads, writes)
        return tok

    def alias(self, srcs, dsts):
        for d in dsts:
            for s_ in srcs:
                if s_.w is not None:
                    self._addr(d, s_.w)
                for t in s_.r.values():
                    self._addr(d, t)


class Ring:
    def __init__(self, nc, name, shape, dtype, n):
        self.t = [nc.alloc_sbuf_tensor(f"{name}{i}", shape, dtype) for i in range(n)]
        self.b = [Buf(f"{name}{i}") for i in range(n)]
        self.i = 0
        self.n = n

    def get(self):
        i = self.i
        self.i = (self.i + 1) % self.n
        return self.t[i], self.b[i]


class Group:
    def __init__(self, kind, blocks, c0, n, first_chain=False):
        self.kind = kind
        self.blocks = blocks
        self.c0 = c0
        self.n = n
        self.first_chain = first_chain


def build_program(nblk, tiles, with_samples=True, depth=2, nslot=5):
    nc = bass.Bass("TRN2", target_bir_lowering=False)
    S = Sched(nc)
    L = depth
    NBS = max(nb + (2 if (hs and with_samples) else 0) for nb, hs in tiles)
    TT = max(nb * 128 + (NS if hs else 0) for nb, hs in tiles)
    TT = max([TT] + [128 * (nb + 1) for nb, hs in tiles if hs and with_samples])
    TT = (TT + 7) // 8 * 8

    def din(name, shape, dt=F32):
        return nc.dram_tensor(name, list(shape), dt, kind="ExternalInput").ap()

    def dout(name, shape, dt=F32):
        return nc.dram_tensor(name, list(shape), dt, kind="ExternalOutput").ap()

    xin = din("xin", [nblk * 128, D])
    y = dout("y", [nblk * 128, D])
    w_in = din("w_in", [L, D, DIN])
    conv_w = din("conv_w", [L, 3, D])
    w_conv_out = din("w_conv_out", [L, D, D])
    w_pool = din("w_pool", [L, 4, 256, 256])
    pool_scale = din("pool_scale", [L, D])
    attn_sinks = din("attn_sinks", [L, 16])
    w_attn_out = din("w_attn_out", [L, D, D])
    w_mix_out = din("w_mix_out", [L, D, D])
    g_pre_mix = din("norm_pre_mix", [L, D])
    g_post_mix = din("norm_post_mix", [L, D])
    g_pre_mlp = din("norm_pre_mlp", [L, D])
    g_post_mlp = din("norm_post_mlp", [L, D])
    w_up = din("w_up", [L, D, DFF])
    w_down = din("w_down", [L, DFF, D])
    rope_cos = din("rope_cos", [128, nblk * 128])
    rope_sin = din("rope_sin", [128, nblk * 128])
    c_maskc = din("c_maskc", [128, 512])
    c_maskp = din("c_maskp", [128, 512])
    c_rot = din("c_rot", [128, 128])
    c_ident = din("c_ident", [128, 128])
    c_invcnt = din("c_invcnt", [128, 8 * 16])
    pconv_o = dout("pconv_o", [L, 2, D])
    ppool_o = dout("ppool_o", [L, 15, D])
    pk_o = dout("pk_o", [L, 128, 256])
    pv_o = dout("pv_o", [L, 128, 256])
    if with_samples:
        xs = din("xs", [NS, D])
        ys = dout("ys", [NS, D])
        sconv = din("sconv", [L, NS, 2, D])
        spool = din("spool", [L, NS, 15, D])
        ck = din("ck", [L, NS, 128, 256])
        cv = din("cv", [L, NS, 128, 256])
        rope_s = din("rope_s", [128, 2])
        sconv_o = dout("sconv_o", [L, NS, 2, D])
        spool_o = dout("spool_o", [L, NS, 15, D])
        sk_o = dout("sk_o", [L, NS, 128, 256])
        sv_o = dout("sv_o", [L, NS, 128, 256])

    sb = nc.alloc_sbuf_tensor
    x = sb("x", [128, NBS, D], F32)
    xB = [Buf(f"x{b}") for b in range(NBS)]
    xsem = [DmaSem(nc, f"dx{b}") for b in range(NBS)]
    hT = sb("hT", [128, 8, TT], BF16)
    big = sb("big", [128, 32 * TT], BF16)
    FT = big[:, :].rearrange("p (f t) -> p f t", f=32)
    acc = big[:, 0:16 * TT].bitcast(F32).rearrange("p (c t) -> p c t", c=8)
    aT = big[:, 16 * TT:24 * TT].rearrange("p (c t) -> p c t", c=8)
    QT = big[:, 24 * TT:32 * TT].rearrange("p (c t) -> p c t", c=8)
    MAXG = 3
    aT_flat = big[:, 16 * TT:24 * TT]
    acc_flat = big[:, 0:16 * TT]
    NHT = (8 * TT) // 1024
    rs_keep = sb("rs_keep", [128, 8], F32)
    rskB = [Buf(f"rsk{b}") for b in range(8)]
    htmQB = [Buf(f"htmq{b}") for b in range(NHT)]
    junkQB = [Buf(f"junkq{b}") for b in range(NHT)]
    hTB = [Buf(f"hT{g}") for g in range(MAXG)]
    accB = [Buf(f"acc{g}") for g in range(MAXG)]
    aTB = [Buf(f"aT{g}") for g in range(MAXG)]
    QTB = [Buf(f"QT{g}") for g in range(MAXG)]
    FTB = [Buf(f"FT{g}") for g in range(MAXG)]

    wslot = [sb(f"wslot{i}", [128, 8, 512], BF16) for i in range(nslot)]
    wB = [Buf(f"w{i}") for i in range(nslot)]
    wsem = [DmaSem(nc, f"dw{i}") for i in range(nslot)]

    ps = nc.alloc_psum_tensor("ps", [128, 8, 512], F32)
    pB = [Buf(f"ps{i}", excl=True) for i in range(8)]

    identb = sb("identb", [128, 128], BF16)
    identf = sb("identf", [128, 128], F32)
    rotb = sb("rotb", [128, 128], BF16)
    maskc = sb("maskc", [128, 512], BF16)
    maskp = sb("maskp", [128, 512], BF16)
    invcnt = sb("invcnt", [128, 8, 16], F32)
    gpreT = sb("gpreT", [128, L, 8], F32)
    gmlpT = sb("gmlpT", [128, L, 8], F32)
    pscT = sb("pscT", [128, L, 8], F32)
    cwT = sb("cwT", [128, L, 3, 8], F32)
    sinkexp = sb("sinkexp", [128, L, 16], F32)
    gpost = sb("gpost", [128, 2, D], F32)
    chalf = sb("chalf", [128, 1], F32)
    constB = Buf("const")
    gpostB = Buf("gpost")
    gpsem = DmaSem(nc, "dgp")
    csem = [DmaSem(nc, f"dc{i}") for i in range(4)]

    cstate = sb("cstate", [128, L, 8, 2], F32)
    pstate = sb("pstate", [128, L, 8, 15], F32)
    cstB = [[Buf(f"cst{l}_{j}") for j in range(8)] for l in range(L)]
    pstB = [[Buf(f"pst{l}_{j}") for j in range(8)] for l in range(L)]
    KTbuf = [sb(f"KT{l}", [128, 4, 640], BF16) for l in range(L)]
    KTB = [Buf(f"KT{l}") for l in range(L)]
    Vbuf = [sb(f"V{l}", [128, 5, 4, 65], BF16) for l in range(L)]
    VB = [[Buf(f"V{l}_{i}") for i in range(5)] for l in range(L)]
    kfin = sb("kfin", [128, 4, 128], F32)
    kfinB = Buf("kfin")
    vfin = sb("vfin", [128, 256], F32)
    vfinB = Buf("vfin")
    if with_samples:
        scst = sb("scst", [128, 8, 32], F32)
        scstB = Buf("scst")
        unew = sb("unew", [128, 8, NS], F32)
        unewB = Buf("unew")
        upnew = sb("upnew", [128, 8, NS], F32)
        upnewB = Buf("upnew")
        kfs = sb("kfs", [128, 4, NS], F32)
        kfsB = Buf("kfs")
        ones64 = sb("ones64", [128, 64], BF16)
        sinkT = sb("sinkT", [128, L, 8], F32)
        spB_ = Buf("spstate")
        skoB = [Buf(f"sko{l}") for l in range(L)]
        svoB = [Buf(f"svo{l}") for l in range(L)]
        ssem = [DmaSem(nc, f"dsm{i}") for i in range(10)]

    R528 = Ring(nc, "r528_", [128, 528], F32, 6)
    RB512 = Ring(nc, "rb512_", [128, 512], BF16, 9)
    RB1024 = Ring(nc, "rb1024_", [128, 1024], BF16, 3)
    RF1024 = Ring(nc, "rf1024_", [128, 1024], F32, 1)
    RS = Ring(nc, "rs_", [128, 4], F32, 16)
    RS16 = Ring(nc, "rs16_", [128, 16], F32, 4)
    ropeT = sb("ropeT", [128, 2, 512], F32)
    ropeB = Buf("rope")
    ropesem = DmaSem(nc, "drope")
    osem = [DmaSem(nc, f"do{i}") for i in range(4)]
    final_toks = []

    pstate_rr = {"i": 0, "avoid": ()}

    def bank():
        i = pstate_rr["i"]
        while i in pstate_rr["avoid"]:
            i = (i + 1) % 8
        pstate_rr["i"] = (i + 1) % 8
        return i

    def bank_pair():
        i = pstate_rr["i"]
        if i % 2:
            i = (i + 1) % 8
        pstate_rr["i"] = (i + 2) % 8
        return i

    def mm(out, lhsT, rhs, start, stop, reads, writes, signal, **kw):
        return S.op("pe", lambda e: e.matmul(out, lhsT, rhs, start=start, stop=stop, **kw),
                    reads=reads, writes=writes, signal=signal)

    def tr(out, in_, ident, reads, writes, signal):
        return S.op("pe", lambda e: e.transpose(out, in_, ident), reads=reads, writes=writes, signal=signal)

    def act(out, in_, func, reads, writes, **kw):
        return S.op("act", lambda e: e.activation(out=out, in_=in_, func=func, **kw), reads=reads, writes=writes)

    def tt(eng, out, in0, in1, op, reads, writes):
        return S.op(eng, lambda e: e.tensor_tensor(out=out, in0=in0, in1=in1, op=op), reads=reads, writes=writes)

    def ts(eng, out, in0, s1, s2, op0, op1, reads, writes):
        if op1 is None:
            return S.op(eng, lambda e: e.tensor_scalar(out=out, in0=in0, scalar1=s1, scalar2=None, op0=op0),
                        reads=reads, writes=writes)
        return S.op(eng, lambda e: e.tensor_scalar(out=out, in0=in0, scalar1=s1, scalar2=s2, op0=op0, op1=op1),
                    reads=reads, writes=writes)

    def stt(out, in0, scalar, in1, op0, op1, reads, writes):
        return S.op("dve", lambda e: e.scalar_tensor_tensor(out=out, in0=in0, scalar=scalar, in1=in1, op0=op0, op1=op1),
                    reads=reads, writes=writes)

    def cp(eng, out, in_, reads, writes):
        if eng == "act":
            return act(out, in_, AF.Copy, reads, writes)
        return S.op(eng, lambda e: e.tensor_copy(out=out, in_=in_), reads=reads, writes=writes)

    preloaded = set()
    for b in range(tiles[0][0]):
        S.dma("sp", xsem[b], x[:, b, :], xin[b * 128:(b + 1) * 128, :], writes=[xB[b]])
        preloaded.add((0, b))

    def load_const_f32(dst, src, sem, **kw):
        S.dma("sp", sem, dst, src, writes=[constB], **kw)

    tmpc, tmpcB = R528.get()
    load_const_f32(identf[:, :], c_ident[:, :], csem[0])
    cp("dve", identb[:, :], identf[:, :], [constB], [constB])
    S.dma("sp", csem[1], tmpc[:, 0:128], c_rot[:, :], writes=[tmpcB])
    cp("dve", rotb[:, :], tmpc[:, 0:128], [tmpcB], [constB])
    tmpc2, tmpc2B = R528.get()
    S.dma("sp", csem[2], tmpc2[:, 0:512], c_maskc[:, :], writes=[tmpc2B])
    cp("dve", maskc[:, :], tmpc2[:, 0:512], [tmpc2B], [constB])
    tmpc3, tmpc3B = R528.get()
    S.dma("sp", csem[3], tmpc3[:, 0:512], c_maskp[:, :], writes=[tmpc3B])
    cp("dve", maskp[:, :], tmpc3[:, 0:512], [tmpc3B], [constB])
    load_const_f32(invcnt[:, :, :].rearrange("p a b -> p (a b)"), c_invcnt[:, :], csem[0])
    for l in range(L):
        load_const_f32(gpreT[:, l, :], g_pre_mix[l, :].rearrange("(c p) -> p c", p=128), csem[1],
                       allow_slow_non_contiguous=True)
        load_const_f32(gmlpT[:, l, :], g_pre_mlp[l, :].rearrange("(c p) -> p c", p=128), csem[2],
                       allow_slow_non_contiguous=True)
        load_const_f32(pscT[:, l, :], pool_scale[l, :].rearrange("(c p) -> p c", p=128), csem[3],
                       allow_slow_non_contiguous=True)
        for t in range(3):
            load_const_f32(cwT[:, l, t, :], conv_w[l, t, :].rearrange("(c p) -> p c", p=128), csem[0],
                           allow_slow_non_contiguous=True)
        load_const_f32(sinkexp[:, l, :], attn_sinks[l:l + 1, :].partition_broadcast(128).rearrange("p a b -> p (a b)"),
                       csem[1])
    act(sinkexp[:, :, :], sinkexp[:, :, :], AF.Exp, [constB], [constB])
    S.op("dve", lambda e: e.memset(chalf[:, :], -0.5), writes=[constB])
    if with_samples:
        S.op("dve", lambda e: e.memset(ones64[:, :], 1.0), writes=[constB])
        for l in range(L):
            for half in range(2):
                load_const_f32(sinkT[half * 64:(half + 1) * 64, l, :],
                               attn_sinks[l, :].rearrange("(c two) -> two c", two=2)[half:half + 1, :]
                               .partition_broadcast(64).rearrange("p a b -> p (a b)"),
                               csem[2 + half], allow_slow_non_contiguous=True)
        act(sinkT[:, :, :], sinkT[:, :, :], AF.Exp, [constB], [constB])
    S.op("dve", lambda e: e.memset(cstate[:, :, :, :].rearrange("p a b c -> p (a b c)"), 0.0),
         writes=[b for l in range(L) for b in cstB[l]])
    S.op("dve", lambda e: e.memset(pstate[:, :, :, :].rearrange("p a b c -> p (a b c)"), 0.0),
         writes=[b for l in range(L) for b in pstB[l]])
    for l in range(L):
        S.op("dve", lambda e: e.memset(Vbuf[l][:, :, :, :].rearrange("p a b c -> p (a b c)"), 1.0), writes=VB[l])
        S.op("dve", lambda e: e.memset(KTbuf[l][:, :, :].rearrange("p a b -> p (a b)"), 0.0), writes=[KTB[l]])

    if with_samples:
        for l in range(L):
            S.dma("sp", ssem[2], sk_o[l, :, 0:127, :], ck[l, :, 1:128, :], writes=[skoB[l]])
            S.dma("sp", ssem[3], sv_o[l, :, 0:127, :], cv[l, :, 1:128, :], writes=[svoB[l]])

    def piece_list(l):
        P = []
        for h in range(2):
            P += [("hc", h), ("cg", h), ("bg", h)]
        for h in range(2):
            P += [("wco", h), ("gc", h)]
        P += [("up", 0), ("up", 1), ("wpool", 0), ("gp", 0), ("gp", 1)]
        P += [("q", 0), ("q", 1), ("kv", 0), ("kdup", 0)]
        for h in range(2):
            P += [("wao", h), ("ga", h)]
        P += [("wmix", 0), ("wmix", 1)]
        P += [("wup", i) for i in range(8)]
        P += [("wdn", (cgi, kq)) for cgi in range(2) for kq in range(4)]
        return [(l, k, i) for (k, i) in P]

    pieces = []
    for ti in range(len(tiles)):
        for l in range(L):
            pieces += [(ti,) + p for p in piece_list(l)]
    pidx = {p: i for i, p in enumerate(pieces)}
    issued = {"n": 0}
    COL0 = {"hc": 0, "bg": 1024, "cg": 2048, "up": 3072, "q": 4096, "gc": 5632, "gp": 6656, "ga": 7680}

    def issue_piece(i):
        ti, l, kind, idx = pieces[i]
        s = i % nslot
        dst = wslot[s]

        def d(out, in_):
            S.dma("pool", wsem[s], out, in_, writes=[wB[s]])
        if kind in COL0:
            c0 = COL0[kind] + idx * 512
            d(dst[:, :, :], w_in[l, :, c0:c0 + 512].rearrange("(kc p) c -> p kc c", p=128))
        elif kind == "kdup":
            pass
        elif kind == "kv":
            d(dst[:, :, :], w_in[l, :, 5120:5632].rearrange("(kc p) c -> p kc c", p=128))
        elif kind in ("wco", "wao", "wmix"):
            wsrc = {"wco": w_conv_out, "wao": w_attn_out, "wmix": w_mix_out}[kind]
            d(dst[:, :, :], wsrc[l, :, idx * 512:(idx + 1) * 512].rearrange("(kc p) c -> p kc c", p=128))
        elif kind == "wpool":
            d(dst[:, :, 0:256], w_pool[l].rearrange("g (kc p) c -> p (g kc) c", p=128))
        elif kind == "wup":
            d(dst[:, :, :], w_up[l, :, idx * 512:(idx + 1) * 512].rearrange("(kc p) c -> p kc c", p=128))
        elif kind == "wdn":
            cgi, kq = idx
            d(dst[:, :, :], w_down[l, kq * 1024:(kq + 1) * 1024, cgi * 512:(cgi + 1) * 512]
              .rearrange("(kc p) c -> p kc c", p=128))
        else:
            raise ValueError(kind)

    def need(ti, l, names):
        idxs = [pidx[(ti, l) + nm] for nm in names]
        lo = min(idxs)
        hi = min(len(pieces), lo + nslot)
        assert max(idxs) < hi, "ring too small"
        while issued["n"] < hi:
            issue_piece(issued["n"])
            issued["n"] += 1
        return {nm: (wslot[i % nslot], wB[i % nslot]) for nm, i in zip(names, idxs)}

    def rstd_from_ss(ss_ap, ssB):
        r, rB = RS.get()
        ts("pool", r[:, 0:1], ss_ap, 1.0 / D, EPS, ALU.mult, ALU.add, [ssB], [rB])
        tt("pool", r[:, 1:2], r[:, 0:1], chalf[:, 0:1], ALU.pow, [rB, constB], [rB])
        return r[:, 1:2], rB

    def norm_c1(slot, np_):
        junk, junkB = acc_flat[:, slot * 1024:(slot + 1) * 1024], junkQB[slot]
        ssT, ssB = RS.get()
        act(junk[0:np_, :], x[0:np_, slot, :], AF.Square, [xB[slot]], [junkB, ssB], accum_out=ssT[0:np_, 0:1])
        return rstd_from_ss_p(ssT, ssB, np_)

    def norm_c1_keep(slot, np_):
        junk, junkB = RB1024.get()
        ssT, ssB = RS.get()
        act(junk[0:np_, :], x[0:np_, slot, :], AF.Square, [xB[slot]], [junkB, ssB], accum_out=ssT[0:np_, 0:1])
        r, rB = RS.get()
        ts("pool", r[0:np_, 0:1], ssT[0:np_, 0:1], 1.0 / D, EPS, ALU.mult, ALU.add, [ssB], [rB])
        tt("pool", rs_keep[0:np_, slot:slot + 1], r[0:np_, 0:1], chalf[0:np_, 0:1], ALU.pow, [rB, constB], [rskB[slot]])

    def norm_c2(slot, np_, rs, rsB):
        htm, htmB = aT_flat[:, slot * 1024:(slot + 1) * 1024], htmQB[slot]
        act(htm[0:np_, :], x[0:np_, slot, :], AF.Identity, [xB[slot], rsB], [htmB], scale=rs[0:np_, :])
        return htm, htmB

    def norm_chain(slot, np_):
        rs, rsB = norm_c1(slot, np_)
        return norm_c2(slot, np_, rs, rsB)

    def norm_tr(htm, htmB, gi, col, np_, gT, l, bk=None):
        if bk is None:
            bk = bank()
        pbf = ps[:, bk, :].bitcast(BF16).rearrange("p (c t) -> p c t", c=8)
        for c in range(8):
            tr(pbf[:, c, 0:np_], htm[0:np_, c * 128:(c + 1) * 128], identb[0:np_, 0:np_],
               [htmB, constB], [pB[bk]], signal=(c == 7))
        tt("dve", hT[:, :, col:col + np_], pbf[:, :, 0:np_],
           gT[:, l, :].unsqueeze(2).to_broadcast([128, 8, np_]), ALU.mult,
           [pB[bk], constB], [hTB[gi]])

    NORM_DL = 3

    def norm_to_hT(groups, gT, l, kept=False):
        S.alias(aTB, htmQB)
        S.alias(accB, junkQB)
        blks = blocks_of(groups)
        nbk = len(blks)
        st = {}
        for i in range(nbk + 3):
            if i - 1 >= 0 and i - 1 < nbk:
                gi, g, slot, np_, col = blks[i - 1]
                st[i - 1] = norm_c2(slot, np_, *st[i - 1]) + (gi, col, np_, gT, l)
            if i < nbk:
                gi, g, slot, np_, col = blks[i]
                use_kept = (kept is True) or (kept and slot in kept)
                st[i] = (rs_keep[:, slot:slot + 1], rskB[slot]) if use_kept else norm_c1(slot, np_)
            if i - 3 >= 0 and i - 3 < nbk:
                norm_tr(*st.pop(i - 3))
        S.alias(htmQB, aTB)
        S.alias(junkQB, accB)

    def rstd_from_ss_p(ssT, ssB, np_):
        r, rB = RS.get()
        ts("pool", r[0:np_, 0:1], ssT[0:np_, 0:1], 1.0 / D, EPS, ALU.mult, ALU.add, [ssB], [rB])
        tt("pool", r[0:np_, 1:2], r[0:np_, 0:1], chalf[0:np_, 0:1], ALU.pow, [rB, constB], [rB])
        return r[:, 1:2], rB

    def proj_fm(wt, wBuf, j4, gi, g):
        bk = bank()
        for k in range(8):
            mm(ps[:, bk, 0:g.n], wt[:, k, j4 * 128:(j4 + 1) * 128], hT[:, k, g.c0:g.c0 + g.n],
               k == 0, k == 7, [wBuf, hTB[gi]], [pB[bk]], signal=(k == 7))
        return bk

    def gate_acc(l, gi, g, i, bkY, bkG, mode, scale_ap=None):
        n = g.n
        th, thB = R528.get()
        act(th[:, 0:n], ps[:, bkG, 0:n], AF.Tanh, [pB[bkG]], [thB], scale=0.5)
        dst = acc[:, i, g.c0:g.c0 + n]
        if mode == "set":
            stt(dst, th[:, 0:n], 1.0, ps[:, bkY, 0:n], ALU.add, ALU.mult, [thB, pB[bkY]], [accB[gi]])
        else:
            tg, tgB = R528.get()
            stt(tg[:, 0:n], th[:, 0:n], 1.0, ps[:, bkY, 0:n], ALU.add, ALU.mult, [thB, pB[bkY]], [tgB])
            if scale_ap is not None:
                stt(dst, tg[:, 0:n], scale_ap, dst, ALU.mult, ALU.add, [tgB, accB[gi], constB], [accB[gi]])
            else:
                tt("dve", dst, dst, tg[:, 0:n], ALU.add, [tgB, accB[gi]], [accB[gi]])

    def phase_conv(ti, l, groups):
        has_s = any(g.kind == "s" for g in groups)
        if has_s:
            prep_sample_conv(l)
        for h in range(2):
            W = need(ti, l, [("hc", h), ("cg", h), ("bg", h)])
            for gi, g in enumerate(groups):
                n = g.n
                for j4 in range(4):
                    j = h * 4 + j4
                    bA = proj_fm(*W[("hc", h)], j4, gi, g)
                    bC = proj_fm(*W[("cg", h)], j4, gi, g)
                    bG = proj_fm(*W[("bg", h)], j4, gi, g)
                    hcs, hcsB = R528.get()
                    cp("act", hcs[:, 0:n], ps[:, bA, 0:n], [pB[bA]], [hcsB])
                    ub, ubB = R528.get()
                    c1, c1B = R528.get()
                    if g.kind == "p":
                        cp("act", ub[:, 0:2], cstate[:, l, j, :], [cstB[l][j]], [ubB])
                        tt("dve", ub[:, 2:2 + n], ps[:, bC, 0:n], hcs[:, 0:n], ALU.mult, [pB[bC], hcsB], [ubB])
                        cp("act", cstate[:, l, j, :], ub[:, n:n + 2], [ubB], [cstB[l][j]])
                        act(c1[:, 0:n], ub[:, 0:n], AF.Identity, [ubB, constB], [c1B], scale=cwT[:, l, 0, j:j + 1])
                        stt(c1[:, 0:n], ub[:, 1:1 + n], cwT[:, l, 1, j:j + 1], c1[:, 0:n], ALU.mult, ALU.add,
                            [ubB, c1B, constB], [c1B])
                        stt(c1[:, 0:n], ub[:, 2:2 + n], cwT[:, l, 2, j:j + 1], c1[:, 0:n], ALU.mult, ALU.add,
                            [ubB, c1B, constB], [c1B])
                    else:
                        sample_conv(l, j, n, bC, hcs, hcsB, ub, ubB, c1, c1B)
                    tt("dve", aT[:, j, g.c0:g.c0 + n], ps[:, bG, 0:n], c1[:, 0:n], ALU.mult, [pB[bG], c1B], [aTB[gi]])
        if has_s:
            finish_sample_conv(l)
        for h in range(2):
            W = need(ti, l, [("wco", h), ("gc", h)])
            wco, wcoB = W[("wco", h)]
            for gi, g in enumerate(groups):
                n = g.n
                for i4 in range(4):
                    i = h * 4 + i4
                    bY = bank()
                    for jj in range(8):
                        mm(ps[:, bY, 0:n], wco[:, jj, i4 * 128:(i4 + 1) * 128], aT[:, jj, g.c0:g.c0 + n],
                           jj == 0, jj == 7, [wcoB, aTB[gi]], [pB[bY]], signal=(jj == 7))
                    bGt = proj_fm(*W[("gc", h)], i4, gi, g)
                    gate_acc(l, gi, g, i, bY, bGt, "set")

    def phase_pool(ti, l, groups):
        has_s = any(g.kind == "s" for g in groups)
        if has_s:
            free_slot = max(slot for g in groups for (slot, cb) in g.blocks) + 1
            assert free_slot < NBS
            prep_sample_pool(l, free_slot)
        for h in range(2):
            W = need(ti, l, [("up", h)])
            for gi, g in enumerate(groups):
                n = g.n
                for j4 in range(4):
                    j = h * 4 + j4
                    pg = j // 2
                    win = POOL_WINS[pg]
                    bU = proj_fm(*W[("up", h)], j4, gi, g)
                    if g.kind == "s":
                        sample_pool(l, j, n, bU, gi, g)
                        continue
                    P, PB = R528.get()
                    cp("act", P[:, 0:15], pstate[:, l, j, :], [pstB[l][j]], [PB])
                    cp("act", P[:, 15:15 + n], ps[:, bU, 0:n], [pB[bU]], [PB])
                    cp("act", pstate[:, l, j, :], P[:, n:n + 15], [PB], [pstB[l][j]])
                    W_ = 15 + n
                    src, srcB = P, PB
                    sh = 1
                    while sh < win:
                        dst, dstB = R528.get()
                        tt("dve", dst[:, sh:W_], src[:, sh:W_], src[:, 0:W_ - sh], ALU.add, [srcB], [dstB])
                        src, srcB = dst, dstB
                        sh *= 2
                    stt(aT[:, j, g.c0:g.c0 + n], src[:, 15:15 + n], 1.0 / win, P[:, 15:15 + n], ALU.mult, ALU.subtract,
                        [srcB, PB], [aTB[gi]])
                    if g.first_chain:
                        tf, tfB = R528.get()
                        tt("dve", tf[:, 0:16], src[:, 15:31], invcnt[:, j, :], ALU.mult, [srcB, constB], [tfB])
                        tt("dve", aT[:, j, g.c0:g.c0 + 16], tf[:, 0:16], P[:, 15:31], ALU.subtract, [tfB, PB], [aTB[gi]])
        if has_s:
            finish_sample_pool(l)
        for h in range(2):
            names = [("wpool", 0), ("gp", h)]
            W = need(ti, l, names)
            wp, wpB = W[("wpool", 0)]
            for gi, g in enumerate(groups):
                n = g.n
                for i4 in range(4):
                    i = h * 4 + i4
                    pg, oc = i // 2, i % 2
                    bY = bank()
                    for kc in range(2):
                        mm(ps[:, bY, 0:n], wp[:, pg * 2 + kc, oc * 128:(oc + 1) * 128],
                           aT[:, pg * 2 + kc, g.c0:g.c0 + n], kc == 0, kc == 1, [wpB, aTB[gi]], [pB[bY]],
                           signal=(kc == 1))
                    bGt = proj_fm(*W[("gp", h)], i4, gi, g)
                    gate_acc(l, gi, g, i, bY, bGt, "add", scale_ap=pscT[:, l, i:i + 1])

    def rope_pre(bk, n):
        qb, qbB = RB512.get()
        cp("act", qb[:, 0:n], ps[:, bk, 0:n], [pB[bk]], [qbB])
        return qb, qbB

    def rope_post(g, bk, qb, qbB, dst_ap, dstBufs, n, extra_f32=None, after=None):
        bR = bank()
        mm(ps[:, bR, 0:n], rotb[:, :], qb[:, 0:n], True, True, [qbB, constB], [pB[bR]], signal=True)
        t1, t1B = R528.get()
        t2, t2B = R528.get()
        if g.kind == "p":
            tt("dve", t1[:, 0:n], ps[:, bk, 0:n], ropeT[:, 0, 0:n], ALU.mult, [pB[bk], ropeB], [t1B])
            tt("dve", t2[:, 0:n], ps[:, bR, 0:n], ropeT[:, 1, 0:n], ALU.mult, [pB[bR], ropeB], [t2B])
        else:
            ts("dve", t1[:, 0:n], ps[:, bk, 0:n], ropeT[:, 0, 0:1], None, ALU.mult, None, [pB[bk], ropeB], [t1B])
            ts("dve", t2[:, 0:n], ps[:, bR, 0:n], ropeT[:, 1, 0:1], None, ALU.mult, None, [pB[bR], ropeB], [t2B])
        if extra_f32 is not None:
            ef, efB = extra_f32
            tt("dve", ef, t1[:, 0:n], t2[:, 0:n], ALU.add, [t1B, t2B], [efB])
            cp("act", dst_ap, ef, [efB], dstBufs)
        else:
            tt("dve", dst_ap, t1[:, 0:n], t2[:, 0:n], ALU.add, [t1B, t2B], dstBufs)
        if after is not None:
            after()

    def rope_chunk(l, g, bk, dst_ap, dstBufs, n, extra_f32=None):
        qb, qbB = rope_pre(bk, n)
        rope_post(g, bk, qb, qbB, dst_ap, dstBufs, n, extra_f32=extra_f32)

    def load_rope(g):
        if g.kind == "p":
            cb0 = g.blocks[0][1]
            S.dma("sp", ropesem, ropeT[:, 0, 0:g.n], rope_cos[:, cb0 * 128:cb0 * 128 + g.n], writes=[ropeB])
            S.dma("sp", ropesem, ropeT[:, 1, 0:g.n], rope_sin[:, cb0 * 128:cb0 * 128 + g.n], writes=[ropeB])
        else:
            S.dma("sp", ropesem, ropeT[:, 0, 0:1], rope_s[:, 0:1], writes=[ropeB], allow_slow_non_contiguous=True)
            S.dma("sp", ropesem, ropeT[:, 1, 0:1], rope_s[:, 1:2], writes=[ropeB], allow_slow_non_contiguous=True)

    def phase_attn(ti, l, groups, is_last_tile):
        Wq = need(ti, l, [("q", 0), ("q", 1), ("kv", 0), ("kdup", 0)])
        kd, kdB = Wq[("kdup", 0)]
        wkv, wvB = Wq[("kv", 0)]
        wv = wkv[:, :, 256:512]
        kdv = kd[:, :, :].rearrange("p kc (h r e) -> p kc h r e", h=4, r=2)
        for r in range(2):
            cp("dve", kdv[:, :, :, r, :], wkv[:, :, 0:256].rearrange("p kc (h e) -> p kc h e", h=4), [wvB], [kdB])
        for gi, g in enumerate(groups):
            n = g.n
            load_rope(g)
            pend = None
            for c in range(8):
                bq = proj_fm(*Wq[("q", c // 4)], c % 4, gi, g)
                qb, qbB = rope_pre(bq, n)
                if pend is not None:
                    rope_post(*pend)
                pend = (g, bq, qb, qbB, QT[:, c, g.c0:g.c0 + n], [QTB[gi]], n)
            if g.kind == "s":
                rope_post(*pend)
                pend = None
                sample_attn(ti, l, gi, g, kd, kdB, wv, wvB)
                continue
            last_chain = is_last_tile and gi == len([gg for gg in groups if gg.kind == "p"]) - 1
            for hh in range(4):
                bkk = proj_fm(kd, kdB, hh, gi, g)
                qb, qbB = rope_pre(bkk, n)
                if pend is not None:
                    rope_post(*pend)
                if last_chain:
                    nb_ = len(g.blocks)
                    t3, t3B = R528.get()

                    def fin(hh=hh, t3=t3, t3B=t3B, nb_=nb_):
                        cp("act", kfin[:, hh, :], t3[:, (nb_ - 1) * 128:nb_ * 128], [t3B], [kfinB])
                    pend = (g, bkk, qb, qbB, KTbuf[l][:, hh, 128:128 + n], [KTB[l]], n, (t3[:, 0:n], t3B), fin)
                else:
                    pend = (g, bkk, qb, qbB, KTbuf[l][:, hh, 128:128 + n], [KTB[l]], n)
            rope_post(*pend)
            pend = None
            for bi, (slot, cb) in enumerate(g.blocks):
                bv = bank()
                col = g.c0 + bi * 128
                for k in range(8):
                    mm(ps[:, bv, 0:256], hT[:, k, col:col + 128], wv[:, k, 0:256], k == 0, k == 7,
                       [wvB, hTB[gi]], [pB[bv]], signal=(k == 7))
                cp("act", Vbuf[l][:, 1 + bi, :, 0:64], ps[:, bv, 0:256].rearrange("p (h e) -> p h e", h=4),
                   [pB[bv]], [VB[l][1 + bi]])
                if last_chain and bi == len(g.blocks) - 1:
                    cp("act", vfin[:, :], ps[:, bv, 0:256], [pB[bv]], [vfinB])
            pstate_rr["avoid"] = (4, 5, 6, 7)
            bO = [4, 5, 6, 7]
            pendT = None

            def emit_T(On, OnB, col, gi=gi):
                bT = bank()
                pbf = ps[:, bT, :].bitcast(BF16).rearrange("p (c t) -> p c t", c=8)
                for c in range(8):
                    tr(pbf[:, c, :], On[:, c * 128:(c + 1) * 128], identb[:, :], [OnB, constB], [pB[bT]],
                       signal=(c == 7))
                cp("act", aT[:, :, col:col + 128], pbf[:, :, :], [pB[bT]], [aTB[gi]])
            for bi, (slot, cb) in enumerate(g.blocks):
                col = g.c0 + bi * 128
                has_prev = not (g.first_chain and bi == 0)
                whs = (["prev", "cur"] if has_prev else ["cur"])
                PTs = {}
                for hp in range(2):
                    if hp == 1 and pendT is not None:
                        emit_T(*pendT)
                        pendT = None
                    for which in whs:
                        kc0 = bi * 128 if which == "prev" else 128 + bi * 128
                        bSr = [bank(), bank()]
                        for pi in range(2):
                            hh = 2 * hp + pi
                            for r in range(2):
                                mm(ps[:, bSr[r], pi * 256:(pi + 1) * 256].rearrange("p (a b) -> p a b", a=2),
                                   KTbuf[l][r * 64:(r + 1) * 64, hh, kc0:kc0 + 128],
                                   QT[r * 64:(r + 1) * 64, 2 * hh:2 * hh + 2, col:col + 128],
                                   True, True, [KTB[l], QTB[gi]], [pB[bSr[r]]], signal=(pi == 1))
                        for r in range(2):
                            PT, PTB = RB512.get()
                            act(PT[:, :], ps[:, bSr[r], :], AF.Exp, [pB[bSr[r]]], [PTB], scale=0.125)
                            tt("dve", PT[:, :], PT[:, :], (maskp if which == "prev" else maskc)[:, :], ALU.mult,
                               [PTB, constB], [PTB])
                            PTs[(r, hp, which)] = (PT, PTB)
                for r in range(2):
                    for hp in range(2):
                        hlist = [(2 * hp, r), (2 * hp, r + 2), (2 * hp + 1, r), (2 * hp + 1, r + 2)]
                        for s_i, (hh, gq) in enumerate(hlist):
                            for wi, which in enumerate(whs):
                                PT, PTB = PTs[(r, hp, which)]
                                vslot = bi if which == "prev" else 1 + bi
                                mm(ps[:, bO[hh], gq * 65:gq * 65 + 65], PT[:, s_i * 128:(s_i + 1) * 128],
                                   Vbuf[l][:, vslot, hh, :], wi == 0, wi == len(whs) - 1,
                                   [PTB, VB[l][vslot]], [pB[bO[hh]]], signal=(wi == len(whs) - 1))
                pso = ps[:, 4:8, 0:260].rearrange("p b (g e) -> p b g e", e=65)
                rd, rdB = RS16.get()
                rdv = rd[:, 0:16].rearrange("p (b g) -> p b g", b=4)
                tt("dve", rdv.unsqueeze(3), pso[:, :, :, 64:65],
                   sinkexp[:, l, :].rearrange("p (b g) -> p b g", b=4).unsqueeze(3), ALU.add,
                   [pB[b_] for b_ in bO] + [constB], [rdB])
                rr, rrB = RS16.get()
                S.op("dve", lambda e: e.reciprocal(out=rr[:, 0:16], in_=rd[:, 0:16]), reads=[rdB], writes=[rrB])
                On, OnB = RB1024.get()
                tt("dve", On[:, :].rearrange("p (b g e) -> p b g e", b=4, g=4), pso[:, :, :, 0:64],
                   rr[:, 0:16].rearrange("p (b g) -> p b g", b=4).unsqueeze(3).to_broadcast([128, 4, 4, 64]),
                   ALU.mult, [pB[b_] for b_ in bO] + [rrB], [OnB])
                pendT = (On, OnB, col)
            emit_T(*pendT)
            pendT = None
            pstate_rr["avoid"] = ()
            nb_ = len(g.blocks)
            cp("act", KTbuf[l][:, :, 0:128], KTbuf[l][:, :, nb_ * 128:(nb_ + 1) * 128], [KTB[l]], [KTB[l]])
            cp("act", Vbuf[l][:, 0, :, 0:64], Vbuf[l][:, nb_, :, 0:64], [VB[l][nb_]], [VB[l][0]])
        for h in range(2):
            W = need(ti, l, [("wao", h), ("ga", h)])
            wao, waoB = W[("wao", h)]
            for gi, g in enumerate(groups):
                n = g.n
                for i4 in range(4):
                    i = h * 4 + i4
                    bY = bank()
                    for jj in range(8):
                        mm(ps[:, bY, 0:n], wao[:, jj, i4 * 128:(i4 + 1) * 128], aT[:, jj, g.c0:g.c0 + n],
                           jj == 0, jj == 7, [waoB, aTB[gi]], [pB[bY]], signal=(jj == 7))
                    bGt = proj_fm(*W[("ga", h)], i4, gi, g)
                    gate_acc(l, gi, g, i, bY, bGt, "add")
                    act(QT[:, i, g.c0:g.c0 + n], acc[:, i, g.c0:g.c0 + n], AF.Copy, [accB[gi]], [QTB[gi]], scale=0.5)

    def post_norm_update(slot, np_, halves, gidx, merged=None):
        tm, tmB = RF1024.get()
        if merged is not None:
            ap, bufs = merged
            junk, junkB = RB1024.get()
            sst, sstB = RS.get()
            act(junk[0:np_, :].rearrange("p (a b) -> p a b", a=2), ap, AF.Square, bufs, [junkB, sstB],
                accum_out=sst[0:np_, 0:1])
            rs, rsB = rstd_from_ss_p(sst, sstB, np_)
            stt(tm[0:np_, :].rearrange("p (a b) -> p a b", a=2), ap, rs[0:np_, :],
                gpost[0:np_, gidx, :].rearrange("p (a b) -> p a b", a=2), ALU.mult, ALU.mult,
                bufs + [rsB, gpostB], [tmB])
        else:
            sss = []
            for (ap, bufs) in halves:
                junk, junkB = RB512.get()
                ssT, ssB = RS.get()
                act(junk[0:np_, :], ap, AF.Square, bufs, [junkB, ssB], accum_out=ssT[0:np_, 0:1])
                sss.append((ssT, ssB))
            sst, sstB = RS.get()
            tt("pool", sst[0:np_, 0:1], sss[0][0][0:np_, 0:1], sss[1][0][0:np_, 0:1], ALU.add,
               [sss[0][1], sss[1][1]], [sstB])
            rs, rsB = rstd_from_ss_p(sst, sstB, np_)
            for hi, (ap, bufs) in enumerate(halves):
                stt(tm[0:np_, hi * 512:(hi + 1) * 512], ap, rs[0:np_, :], gpost[0:np_, gidx, hi * 512:(hi + 1) * 512],
                    ALU.mult, ALU.mult, bufs + [rsB, gpostB], [tmB])
        tt("dve", x[0:np_, slot, :], x[0:np_, slot, :], tm[0:np_, :], ALU.add, [xB[slot], tmB], [xB[slot]])

    def blocks_of(groups):
        out = []
        for gi, g in enumerate(groups):
            for bi, (slot, cb) in enumerate(g.blocks):
                np_ = 128 if g.kind == "p" else NS
                col = g.c0 + bi * 128
                out.append((gi, g, slot, np_, col))
        return out

    def pn_b1(np_, ap, bufs):
        junk, junkB = RB1024.get()
        sst, sstB = RS.get()
        act(junk[0:np_, :].rearrange("p (a b) -> p a b", a=2), ap, AF.Square, bufs, [junkB, sstB],
            accum_out=sst[0:np_, 0:1])
        return rstd_from_ss_p(sst, sstB, np_)

    def pn_b2(slot, np_, ap, bufs, rs, rsB, gidx):
        tm, tmB = RF1024.get()
        stt(tm[0:np_, :].rearrange("p (a b) -> p a b", a=2), ap, rs[0:np_, :],
            gpost[0:np_, gidx, :].rearrange("p (a b) -> p a b", a=2), ALU.mult, ALU.mult,
            bufs + [rsB, gpostB], [tmB])
        tt("dve", x[0:np_, slot, :], x[0:np_, slot, :], tm[0:np_, :], ALU.add, [xB[slot], tmB], [xB[slot]])

    def phase_mix(ti, l, groups):
        W = need(ti, l, [("wmix", 0), ("wmix", 1)])
        S.alias(aTB, htmQB)
        S.alias(accB, junkQB)
        blks = blocks_of(groups)
        order = sorted(range(len(groups)), key=lambda gi_: (groups[gi_].kind == "s", -gi_))
        blks = [b_ for gi_ in order for b_ in blks if b_[0] == gi_]
        nbk = len(blks)
        A = {}
        B1 = {}
        C1 = {}
        C2 = {}
        pend = []
        for i in range(nbk + 5):
            if i < nbk:
                gi, g, slot, np_, col = blks[i]
                b0 = 2 * (i % 3)
                for cgi in range(2):
                    wm, wmB = W[("wmix", cgi)]
                    for c in range(8):
                        mm(ps[0:np_, b0 + cgi, :], QT[:, c, col:col + np_], wm[:, c, :], c == 0, c == 7,
                           [wmB, QTB[gi]], [pB[b0 + cgi]], signal=(c == 7))
                A[i] = (ps[0:np_, b0:b0 + 2, :], [pB[b0], pB[b0 + 1]])
            j = i - 5
            if 0 <= j < nbk:
                pend.append(C2.pop(j))
                if i < nbk:
                    norm_tr(*pend.pop(0), bk=6 + (i % 2))
            j = i - 4
            if 0 <= j < nbk:
                gi, g, slot, np_, col = blks[j]
                C2[j] = norm_c2(slot, np_, *C1.pop(j)) + (gi, col, np_, gmlpT, l)
            j = i - 3
            if 0 <= j < nbk:
                gi, g, slot, np_, col = blks[j]
                C1[j] = norm_c1(slot, np_)
            j = i - 1
            if 0 <= j < nbk:
                gi, g, slot, np_, col = blks[j]
                B1[j] = pn_b1(np_, *A[j])
            j = i - 2
            if 0 <= j < nbk:
                gi, g, slot, np_, col = blks[j]
                pn_b2(slot, np_, *A.pop(j), *B1.pop(j), 0)
        return pend

    def phase_mlp(ti, l, groups, pendq=(), final_hook=None, keep_stats=False):
        pendq = list(pendq)
        S.alias(accB + aTB + QTB + htmQB + junkQB, FTB)
        def up_chunks(i, wu, wuB, gi, g):
            n = g.n
            for f4 in range(4):
                bk = proj_fm(wu, wuB, f4, gi, g)
                rl, rlB = RB512.get()
                act(rl[:, 0:n], ps[:, bk, 0:n], AF.Relu, [pB[bk]], [rlB])
                tt("dve", FT[:, i * 4 + f4, g.c0:g.c0 + n], rl[:, 0:n], rl[:, 0:n], ALU.mult, [rlB], [FTB[gi]])

        def flush_pend(gi):
            for e_ in [e_ for e_ in pendq if e_[2] == gi]:
                pendq.remove(e_)
                norm_tr(*e_)
                if not pendq:
                    S.alias(htmQB, FTB)

        NF = 3
        W = need(ti, l, [("wup", i) for i in range(NF)])
        order = sorted(range(len(groups)), key=lambda gi_: (groups[gi_].kind == "s", -gi_))
        for gi in order:
            g = groups[gi]
            flush_pend(gi)
            for i in range(NF):
                up_chunks(i, *W[("wup", i)], gi, g)
        assert not pendq
        for i in range(NF, 8):
            W = need(ti, l, [("wup", i)])
            for gi, g in enumerate(groups):
                up_chunks(i, *W[("wup", i)], gi, g)
        stash = hT[:, :, :].rearrange("p c t -> p (c t)")[:, 0:(8 * TT // 1024) * 1024].bitcast(F32).rearrange("p (b c) -> p b c", c=512)
        stB = [Buf(f"stash{b}") for b in range(NBS)]
        S.alias(hTB, stB)
        blks = blocks_of(groups)
        assert len(blks) <= 8
        for kq in range(4):
            W = need(ti, l, [("wdn", (0, kq))])
            wd, wdB = W[("wdn", (0, kq))]
            for bi_, (gi, g, slot, np_, col) in enumerate(blks):
                for f8 in range(8):
                    f = kq * 8 + f8
                    mm(ps[0:np_, bi_, :], FT[:, f, col:col + np_], wd[:, f8, :], f == 0, f == 31,
                       [wdB, FTB[gi]], [pB[bi_]], signal=(f8 == 7))
        for bi_, (gi, g, slot, np_, col) in enumerate(blks):
            cp("act", stash[0:np_, slot, :], ps[0:np_, bi_, :], [pB[bi_]], [stB[slot]])
        pstate_rr["i"] = 0
        W = need(ti, l, [("wdn", (1, kq)) for kq in range(4)])
        for (gi, g, slot, np_, col) in blks:
            bk = bank()
            for f in range(32):
                wd, wdB = W[("wdn", (1, f // 8))]
                mm(ps[0:np_, bk, :], FT[:, f, col:col + np_], wd[:, f % 8, :], f == 0, f == 31,
                   [wdB, FTB[gi]], [pB[bk]], signal=(f == 31))
            post_norm_update(slot, np_, [(stash[0:np_, slot, :], [stB[slot]]), (ps[0:np_, bk, :], [pB[bk]])], 1)
            if keep_stats:
                norm_c1_keep(slot, np_)
            if final_hook is not None:
                final_hook(g, slot)
        S.alias(stB, hTB)
        S.alias(FTB, accB + aTB + QTB + htmQB + junkQB)

    def tm_rows_out(srcT, srcB, dst_ap, sem):
        bk = bank_pair()
        for j in range(8):
            o = ps[0:NS, bk, j * 128:(j + 1) * 128] if j < 4 else ps[0:NS, bk + 1, (j - 4) * 128:(j - 3) * 128]
            tr(o, srcT[:, j, :], identf[:, :], [srcB, constB], [pB[bk], pB[bk + 1]], signal=True)
        o1, o1B = RF1024.get()
        cp("dve", o1[0:NS, 0:512], ps[0:NS, bk, :], [pB[bk]], [o1B])
        cp("dve", o1[0:NS, 512:1024], ps[0:NS, bk + 1, :], [pB[bk + 1]], [o1B])
        final_toks.append(S.dma("sp", sem, dst_ap, o1[0:NS, :], reads=[o1B]))

    def prep_sample_conv(l):
        st, stB_ = RF1024.get()
        S.dma("sp", ssem[0], st[0:32, :], sconv[l].rearrange("b t d -> (b t) d"), writes=[stB_])
        bk = bank()
        for j in range(8):
            tr(ps[:, bk, j * 32:(j + 1) * 32], st[0:32, j * 128:(j + 1) * 128], identf[0:32, 0:32],
               [stB_, constB], [pB[bk]], signal=(j == 7))
        cp("dve", scst[:, :, :].rearrange("p a b -> p (a b)"), ps[:, bk, 0:256], [pB[bk]], [scstB])

    def sample_conv(l, j, n, bC, hcs, hcsB, ub, ubB, c1, c1B):
        tt("dve", ub[:, 0:n], ps[:, bC, 0:n], hcs[:, 0:n], ALU.mult, [pB[bC], hcsB], [ubB])
        cp("act", unew[:, j, :], ub[:, 0:n], [ubB], [unewB])
        sv_ = scst[:, j, :].rearrange("p (b t) -> p b t", t=2)
        ts("dve", c1[:, 0:n], sv_[:, :, 0], cwT[:, l, 0, j:j + 1], None, ALU.mult, None, [scstB, constB], [c1B])
        stt(c1[:, 0:n], sv_[:, :, 1], cwT[:, l, 1, j:j + 1], c1[:, 0:n], ALU.mult, ALU.add, [scstB, c1B, constB], [c1B])
        stt(c1[:, 0:n], ub[:, 0:n], cwT[:, l, 2, j:j + 1], c1[:, 0:n], ALU.mult, ALU.add, [ubB, c1B, constB], [c1B])

    def finish_sample_conv(l):
        tm_rows_out(unew, unewB, sconv_o[l, :, 1, :], ssem[1])
        final_toks.append(S.dma("sp", ssem[2], sconv_o[l, :, 0, :], sconv[l, :, 1, :]))

    sp_tiles = {}

    def prep_sample_pool(l, free_slot):
        spA, spAB = RF1024.get()
        rows = spool[l].rearrange("b t d -> (b t) d")
        S.dma("sp", ssem[0], spA[0:120, :], rows[0:120, :], writes=[spAB])
        S.dma("sp", ssem[3], x[0:120, free_slot, :], rows[120:240, :], writes=[xB[free_slot]])
        sp_tiles["A"] = (spA, spAB)
        sp_tiles["B"] = (x[:, free_slot, :], xB[free_slot])

    def sample_pool(l, j, n, bU, gi, g):
        cp("act", upnew[:, j, :], ps[:, bU, 0:n], [pB[bU]], [upnewB])
        bk = bank()
        for hlf, key in enumerate(("A", "B")):
            tl, tlB = sp_tiles[key]
            tr(ps[:, bk, hlf * 120:(hlf + 1) * 120], tl[0:120, j * 128:(j + 1) * 128], identf[0:120, 0:120],
               [tlB, constB], [pB[bk]], signal=(hlf == 1))
        stv = ps[:, bk, 0:240].rearrange("p (b t) -> p b t", t=15)
        win = POOL_WINS[j // 2]
        red, redB = R528.get()
        S.op("dve", lambda e: e.tensor_reduce(out=red[:, 0:NS], in_=stv[:, :, 15 - (win - 1):15], axis=AX.X, op=ALU.add),
             reads=[pB[bk]], writes=[redB])
        tt("dve", red[:, 0:NS], red[:, 0:NS], upnew[:, j, :], ALU.add, [redB, upnewB], [redB])
        stt(aT[:, j, g.c0:g.c0 + NS], red[:, 0:NS], 1.0 / win, upnew[:, j, :], ALU.mult, ALU.subtract,
            [redB, upnewB], [aTB[gi]])

    def finish_sample_pool(l):
        tm_rows_out(upnew, upnewB, spool_o[l, :, 14, :], ssem[1])
        final_toks.append(S.dma("sp", ssem[2], spool_o[l, :, 0:14, :], spool[l, :, 1:15, :]))

    def sample_attn(ti, l, gi, g, kd, kdB, wv, wvB):
        n = NS
        c0 = g.c0
        for hh in range(4):
            bkk = proj_fm(kd, kdB, hh, gi, g)
            t3, t3B = R528.get()
            jb, jbB = RB512.get()
            rope_chunk(l, g, bkk, jb[:, 0:n], [jbB], n, extra_f32=(t3[:, 0:n], t3B))
            cp("act", kfs[:, hh, :], t3[:, 0:n], [t3B], [kfsB])
        bk = bank()
        for hh in range(4):
            tr(ps[0:NS, bk, hh * 128:(hh + 1) * 128], kfs[:, hh, :], identf[:, :], [kfsB, constB], [pB[bk]], signal=True)
        knew, knewB = R528.get()
        cp("dve", knew[0:NS, 0:256].rearrange("p (h e) -> p h e", h=4),
           ps[0:NS, bk, :].rearrange("p (h e) -> p h e", h=4)[:, :, 0:64], [pB[bk]], [knewB])
        S.dma("sp", ssem[4], sk_o[l, :, 127, :], knew[0:NS, 0:256], reads=[knewB], writes=[skoB[l]])
        bv = bank()
        for k in range(8):
            mm(ps[0:NS, bv, 0:256], hT[:, k, c0:c0 + NS], wv[:, k, 0:256], k == 0, k == 7,
               [wvB, hTB[gi]], [pB[bv]], signal=(k == 7))
        vnew, vnewB = R528.get()
        cp("act", vnew[0:NS, 0:256], ps[0:NS, bv, 0:256], [pB[bv]], [vnewB])
        S.dma("sp", ssem[5], sv_o[l, :, 127, :], vnew[0:NS, 0:256], reads=[vnewB], writes=[svoB[l]])
        bOT = bank()
        pstate_rr["avoid"] = (bOT,)
        def stage1(b):
            ks, ksB = R528.get()
            S.dma("sp", ssem[6 + b % 2], ks[:, 0:256], sk_o[l, b, :, :], reads=[skoB[l]], writes=[ksB])
            vs, vsB = R528.get()
            S.dma("sp", ssem[8 + b % 2], vs[:, 0:256], sv_o[l, b, :, :], reads=[svoB[l]], writes=[vsB])
            kb, kbB = RB512.get()
            for r in range(2):
                cp("dve", kb[:, :].rearrange("p (h r e) -> p h r e", h=4, r=2)[:, :, r, :],
                   ks[:, 0:256].rearrange("p (h e) -> p h e", h=4), [ksB], [kbB])
            vb, vbB = RB512.get()
            cp("act", vb[:, 0:256], vs[:, 0:256], [vsB], [vbB])
            bt = bank()
            pbf = ps[:, bt, :].bitcast(BF16)
            for hh in range(4):
                tr(pbf[:, hh * 128:(hh + 1) * 128], kb[:, hh * 128:(hh + 1) * 128], identb[:, :], [kbB, constB],
                   [pB[bt]], signal=(hh == 3))
            kts, ktsB = RB512.get()
            cp("act", kts[:, :], pbf[:, 0:512], [pB[bt]], [ktsB])
            return vb, vbB, kts, ktsB

        def stage2(b, vb, vbB, kts, ktsB):
            pts = []
            for r in range(2):
                bS = bank()
                for hh in range(4):
                    mm(ps[:, bS, 2 * hh:2 * hh + 2], kts[r * 64:(r + 1) * 64, hh * 128:(hh + 1) * 128],
                       QT[r * 64:(r + 1) * 64, 2 * hh:2 * hh + 2, c0 + b:c0 + b + 1].rearrange("p a b -> p (a b)"),
                       True, True, [ktsB, QTB[gi]], [pB[bS]], signal=(hh == 3))
                pt, ptB = RB512.get()
                act(pt[:, 0:8], ps[:, bS, 0:8], AF.Exp, [pB[bS]], [ptB], scale=0.125)
                pts.append((pt, ptB))
            for r in range(2):
                pt, ptB = pts[r]
                ov = ps[r * 64:(r + 1) * 64, bOT, 0:128].rearrange("p (c s) -> p c s", s=NS)
                dv = ps[r * 64:(r + 1) * 64, bOT, 128:256].rearrange("p (c s) -> p c s", s=NS)
                for hh in range(4):
                    mm(ov[:, 2 * hh:2 * hh + 2, b], vb[:, hh * 64:(hh + 1) * 64], pt[:, 2 * hh:2 * hh + 2],
                       True, True, [vbB, ptB], [pB[bOT]], signal=False)
                mm(dv[:, :, b], ones64[:, :], pt[:, 0:8], True, True, [constB, ptB], [pB[bOT]], signal=True)

        cur = stage1(0)
        for b in range(NS):
            nxt = stage1(b + 1) if b + 1 < NS else None
            stage2(b, *cur)
            cur = nxt
        pstate_rr["avoid"] = ()
        rd, rdB = R528.get()
        tt("dve", rd[:, 0:128].rearrange("p (c b) -> p c b", c=8), ps[:, bOT, 128:256].rearrange("p (c b) -> p c b", c=8),
           sinkT[:, l, :].unsqueeze(2).to_broadcast([128, 8, NS]), ALU.add, [pB[bOT], constB], [rdB])
        rr, rrB = R528.get()
        S.op("dve", lambda e: e.reciprocal(out=rr[:, 0:128], in_=rd[:, 0:128]), reads=[rdB], writes=[rrB])
        tt("dve", aT[:, :, c0:c0 + NS], ps[:, bOT, 0:128].rearrange("p (c b) -> p c b", c=8),
           rr[:, 0:128].rearrange("p (c b) -> p c b", c=8), ALU.mult, [pB[bOT], rrB], [aTB[gi]])

    def emit_chain_outputs(l):
        bk = bank_pair()
        for j in range(8):
            tr(ps[0:2, bk, j * 128:(j + 1) * 128] if j < 4 else ps[0:2, bk + 1, (j - 4) * 128:(j - 3) * 128],
               cstate[:, l, j, :], identf[:, :], [cstB[l][j], constB], [pB[bk], pB[bk + 1]], signal=True)
        o1, o1B = RF1024.get()
        cp("dve", o1[0:2, 0:512], ps[0:2, bk, :], [pB[bk]], [o1B])
        cp("dve", o1[0:2, 512:1024], ps[0:2, bk + 1, :], [pB[bk + 1]], [o1B])
        final_toks.append(S.dma("sp", osem[0], pconv_o[l, :, :], o1[0:2, :], reads=[o1B]))
        bk = bank_pair()
        for j in range(8):
            tr(ps[0:15, bk, j * 128:(j + 1) * 128] if j < 4 else ps[0:15, bk + 1, (j - 4) * 128:(j - 3) * 128],
               pstate[:, l, j, :], identf[:, :], [pstB[l][j], constB], [pB[bk], pB[bk + 1]], signal=True)
        o2, o2B = RF1024.get()
        cp("dve", o2[0:15, 0:512], ps[0:15, bk, :], [pB[bk]], [o2B])
        cp("dve", o2[0:15, 512:1024], ps[0:15, bk + 1, :], [pB[bk + 1]], [o2B])
        final_toks.append(S.dma("sp", osem[1], ppool_o[l, :, :], o2[0:15, :], reads=[o2B]))
        bk = bank()
        for hh in range(4):
            tr(ps[:, bk, hh * 128:(hh + 1) * 128], kfin[:, hh, :], identf[:, :], [kfinB, constB], [pB[bk]], signal=True)
        o3, o3B = RF1024.get()
        cp("dve", o3[:, 0:256].rearrange("p (h e) -> p h e", h=4),
           ps[:, bk, :].rearrange("p (h e) -> p h e", h=4)[:, :, 0:64], [pB[bk]], [o3B])
        final_toks.append(S.dma("sp", osem[2], pk_o[l, :, :], o3[:, 0:256], reads=[o3B]))
        final_toks.append(S.dma("sp", osem[3], pv_o[l, :, :], vfin[:, :], reads=[vfinB]))

    blk0 = 0
    ntiles = len(tiles)
    statted = {}
    pend_stats = []
    for ti, (nb, hs) in enumerate(tiles):
        groups = []
        b = 0
        ngp = (nb + 3) // 4
        sizes = [nb // ngp + (1 if i < nb % ngp else 0) for i in range(ngp)]
        for m in sizes:
            groups.append(Group("p", [(b + i, blk0 + b + i) for i in range(m)], b * 128, m * 128,
                                first_chain=(ti == 0 and b == 0)))
            b += m
        if hs and with_samples:
            groups.append(Group("s", [(nb, -1)], nb * 128, NS))
        assert len(groups) <= MAXG
        is_last_tile = (blk0 + nb == nblk)
        for b in range(nb):
            if (ti, b) not in preloaded:
                S.dma("sp", xsem[b], x[:, b, :], xin[(blk0 + b) * 128:(blk0 + b + 1) * 128, :], writes=[xB[b]])
        if hs and with_samples:
            S.dma("sp", xsem[nb], x[0:NS, nb, :], xs[:, :], writes=[xB[nb]])
        for l in range(L):
            S.dma("sp", gpsem, gpost[:, 0, :], g_post_mix[l:l + 1, :].partition_broadcast(128).rearrange("p a b -> p (a b)"),
                  writes=[gpostB])
            S.dma("sp", gpsem, gpost[:, 1, :], g_post_mlp[l:l + 1, :].partition_broadcast(128).rearrange("p a b -> p (a b)"),
                  writes=[gpostB])
            norm_to_hT(groups, gpreT, l, kept=(True if l > 0 else statted.get(ti, set())))
            phase_conv(ti, l, groups)
            phase_pool(ti, l, groups)
            phase_attn(ti, l, groups, is_last_tile)
            pq = phase_mix(ti, l, groups)
            hook = None
            if l == L - 1:
                def hook(g, slot, blk0=blk0, nb=nb, ti=ti):
                    if g.kind == "p":
                        final_toks.append(S.dma("sp", xsem[slot], y[(blk0 + slot) * 128:(blk0 + slot + 1) * 128, :],
                                                x[:, slot, :], reads=[xB[slot]]))
                        if ti + 1 < ntiles and slot < tiles[ti + 1][0]:
                            nb0 = blk0 + nb
                            S.dma("sp", xsem[slot], x[:, slot, :], xin[(nb0 + slot) * 128:(nb0 + slot + 1) * 128, :],
                                  writes=[xB[slot]])
                            preloaded.add((ti + 1, slot))
                            pend_stats.append(slot)
                            if len(pend_stats) > 2:
                                s_ = pend_stats.pop(0)
                                norm_c1_keep(s_, 128)
                                statted.setdefault(ti + 1, set()).add(s_)
                    else:
                        final_toks.append(S.dma("sp", xsem[slot], ys[:, :], x[0:NS, slot, :], reads=[xB[slot]]))
            phase_mlp(ti, l, groups, pq, hook, keep_stats=(l < L - 1))
            if l == L - 1:
                while pend_stats:
                    s_ = pend_stats.pop(0)
                    norm_c1_keep(s_, 128)
                    statted.setdefault(ti + 1, set()).add(s_)
            if is_last_tile:
                emit_chain_outputs(l)
        blk0 += nb
    assert blk0 == nblk
    for t in final_toks:
        S._wait("sp", t, True)
    return nc, S


def host_consts(nblk, p0):
    half = 32
    inv = (10000.0 ** (-np.arange(half, dtype=np.float32) / np.float32(half))).astype(np.float32)
    pos = (p0 + np.arange(nblk * 128)).astype(np.float32)
    fidx = np.arange(128) % 32
    ang = pos[None, :] * inv[fidx][:, None]
    cos = np.cos(ang).astype(np.float32)
    sin = np.sin(ang).astype(np.float32)
    k = np.arange(128)[:, None]
    q = np.arange(128)[None, :]
    maskc = np.tile((k <= q).astype(np.float32), (1, 4))
    maskp = np.tile((k > q).astype(np.float32), (1, 4))
    rot = np.zeros((128, 128), np.float32)
    for m in range(128):
        if m % 64 < 32:
            rot[m + 32, m] = -1.0
        else:
            rot[m - 32, m] = 1.0
    ident = np.eye(128, dtype=np.float32)
    invcnt = np.zeros((128, 8, 16), np.float32)
    for j in range(8):
        win = POOL_WINS[j // 2]
        invcnt[:, j, :] = 1.0 / np.minimum(win, np.arange(16) + 1).astype(np.float32)[None, :]
    return {"rope_cos": cos, "rope_sin": sin, "c_maskc": maskc, "c_maskp": maskp, "c_rot": rot,
            "c_ident": ident, "c_invcnt": invcnt.reshape(128, 128)}


def sample_inputs(inp, c):
    sl = slice(c * NS, (c + 1) * NS)
    half = 32
    inv = (10000.0 ** (-np.arange(half, dtype=np.float32) / np.float32(half))).astype(np.float32)
    ang = np.float32(16384.0) * inv[np.arange(128) % 32]
    rs = np.stack([np.cos(ang), np.sin(ang)], axis=1).astype(np.float32)
    return {"xs": np.ascontiguousarray(inp["x_sample"][sl, 0, :]),
            "sconv": np.ascontiguousarray(inp["state_conv"][:, sl]),
            "spool": np.ascontiguousarray(inp["state_pool"][:, sl]),
            "ck": np.ascontiguousarray(inp["cache_k"][:, sl].reshape(2, NS, 128, 256)),
            "cv": np.ascontiguousarray(inp["cache_v"][:, sl].reshape(2, NS, 128, 256)),
            "rope_s": rs}


FULL_TILES = [(7, False), (7, False), (7, False), (7, False), (5, True)]
WEIGHT_KEYS = ["w_in", "conv_w", "w_conv_out", "w_pool", "pool_scale", "attn_sinks", "w_attn_out", "w_mix_out",
               "norm_pre_mix", "norm_post_mix", "norm_pre_mlp", "norm_post_mlp", "w_up", "w_down"]
_CACHE = {}


def kernel(**inputs):
    inp = {k: np.ascontiguousarray(np.asarray(v)) for k, v in inputs.items()}
    nblk = NBLK_FULL
    if "prog" not in _CACHE:
        _CACHE["prog"] = build_program(nblk, FULL_TILES, with_samples=True)
    nc, _ = _CACHE["prog"]
    xp = inp["x_prompt"]
    in_maps = []
    for c in range(NCORES):
        s, h = c // 2, c % 2
        b0 = 0 if h == 0 else 64 - nblk
        m = {"xin": np.ascontiguousarray(xp[s, b0 * 128:(b0 + nblk) * 128, :])}
        for k in WEIGHT_KEYS:
            m[k] = inp[k]
        m.update(host_consts(nblk, b0 * 128))
        m.update(sample_inputs(inp, c))
        in_maps.append(m)
    res = run_bass_kernel_spmd(nc, in_maps, core_ids=list(range(NCORES)))
    R = res.results
    B = xp.shape[0]
    y_prompt = np.zeros((B, SEQ, D), np.float32)
    pc = np.zeros((2, B, 2, D), np.float32)
    pp = np.zeros((2, B, 15, D), np.float32)
    pk = np.zeros((2, B, 128, 4, 64), np.float32)
    pv = np.zeros((2, B, 128, 4, 64), np.float32)
    for c in range(NCORES):
        s, h = c // 2, c % 2
        yc = R[c]["y"]
        if h == 0:
            y_prompt[s, 0:nblk * 128] = yc
        else:
            b0 = 64 - nblk
            y_prompt[s, nblk * 128:] = yc[(nblk - b0) * 128:]
            pc[:, s] = R[c]["pconv_o"]
            pp[:, s] = R[c]["ppool_o"]
            pk[:, s] = R[c]["pk_o"].reshape(2, 128, 4, 64)
            pv[:, s] = R[c]["pv_o"].reshape(2, 128, 4, 64)
    ys = np.zeros_like(inp["x_sample"])
    sc = np.zeros_like(inp["state_conv"])
    sp_ = np.zeros_like(inp["state_pool"])
    sk = np.zeros_like(inp["cache_k"])
    sv = np.zeros_like(inp["cache_v"])
    for c in range(NCORES):
        sl = slice(c * NS, (c + 1) * NS)
        ys[sl, 0, :] = R[c]["ys"]
        sc[:, sl] = R[c]["sconv_o"]
        sp_[:, sl] = R[c]["spool_o"]
        sk[:, sl] = R[c]["sk_o"].reshape(2, NS, 128, 4, 64)
        sv[:, sl] = R[c]["sv_o"].reshape(2, NS, 128, 4, 64)
    return (y_prompt, ys, pc, pp, pk, pv, sc, sp_, sk, sv)
```
